# Optimizing a Trainium2 kernel written in Bass

```python
import jax, jax.numpy as jnp
from jax import lax
import numpy as np

D_MODEL = 1024
BATCH = 8
SEQ = 2048
DEPTH = 1
DEC_BATCH = 128
DEC_SEQ = 8
PAST_LEN = 16384
PAGE_SIZE = 128

MIX_WIDTH = D_MODEL
CONV_DIM = MIX_WIDTH // 2
CONV_GROUPS = 8
CONV_K = 3
HEADS_B = 8
HEAD_DIM_B = (MIX_WIDTH - CONV_DIM) // HEADS_B
CHUNK_DIM = HEADS_B * HEAD_DIM_B
CHUNK = 128
D_FF = ((8 * D_MODEL // 3 + 127) // 128) * 128
PLE_DIM = 256
IN_COLS = 3 * CONV_DIM + 2 * CHUNK_DIM
EPS = 1e-6

kernel_name = "hymba_conv_gmlp_macaron_step"


def rmsnorm(x, g):
    xf = x.astype(jnp.float32)
    y = xf * lax.rsqrt(jnp.mean(xf * xf, axis=-1, keepdims=True) + EPS)
    return (y * g.astype(jnp.float32)).astype(x.dtype)


def swiglu(x, w_gate, w_up, w_down):
    return (jax.nn.silu(x @ w_gate) * (x @ w_up)) @ w_down


def causal_dwconv(prev, z, w):
    L = z.shape[1]
    full = jnp.concatenate([prev, z], axis=1)
    y = sum(w[k] * full[:, k:k + L] for k in range(CONV_K))
    return y, full[:, full.shape[1] - (CONV_K - 1):]


def chunk_spatial_mix(v, w_s, b_s):
    B, L, H, D = v.shape
    n_chunks = -(-L // CHUNK)
    pad = n_chunks * CHUNK - L
    vp = jnp.pad(v, ((0, 0), (0, pad), (0, 0), (0, 0))).reshape(B, n_chunks, CHUNK, H, D)
    causal = jnp.tril(jnp.ones((CHUNK, CHUNK), dtype=bool))
    wm = jnp.where(causal[None], w_s, jnp.zeros_like(w_s))
    out = jnp.einsum('hts,bcshd->bcthd', wm, vp) + b_s.T[None, None, :, :, None]
    return out.reshape(B, n_chunks * CHUNK, H, D)[:, :L]


def layer(x, p_emb, conv_prev,
          ffn1_pre_g, ffn1_post_g, ffn1_w_gate, ffn1_w_up, ffn1_w_down,
          mix_pre_g, mix_post_g, w_in, conv_w, v_norm_g, w_s, b_s, out_g_a, out_g_b, w_out,
          ffn2_pre_g, ffn2_post_g, ffn2_w_gate, ffn2_w_up, ffn2_w_down,
          ple_w_gate, ple_w_proj, ple_post_g):
    B, L, _ = x.shape
    h = x + 0.5 * rmsnorm(swiglu(rmsnorm(x, ffn1_pre_g), ffn1_w_gate, ffn1_w_up, ffn1_w_down), ffn1_post_g)
    n = rmsnorm(h, mix_pre_g)
    proj = n @ w_in
    o = 0
    b_a = proj[..., o:o + CONV_DIM]; o += CONV_DIM
    c_a = proj[..., o:o + CONV_DIM]; o += CONV_DIM
    h_a = proj[..., o:o + CONV_DIM]; o += CONV_DIM
    u = proj[..., o:o + CHUNK_DIM]; o += CHUNK_DIM
    v = proj[..., o:o + CHUNK_DIM]
    conv_out, new_conv = causal_dwconv(conv_prev, c_a * h_a, conv_w)
    y_a = b_a * conv_out
    v_n = rmsnorm(v.reshape(B, L, HEADS_B, HEAD_DIM_B), v_norm_g)
    mixed = chunk_spatial_mix(v_n, w_s, b_s)
    y_b = (u.reshape(B, L, HEADS_B, HEAD_DIM_B) * mixed).reshape(B, L, CHUNK_DIM)
    n_cur = ((L - 1) % CHUNK) + 1
    v_cur = v_n[:, L - n_cur:]
    y = jnp.concatenate([rmsnorm(y_a, out_g_a), rmsnorm(y_b, out_g_b)], axis=-1) @ w_out
    h = h + rmsnorm(y, mix_post_g)
    h = h + 0.5 * rmsnorm(swiglu(rmsnorm(h, ffn2_pre_g), ffn2_w_gate, ffn2_w_up, ffn2_w_down), ffn2_post_g)
    gate = jax.nn.sigmoid(h @ ple_w_gate)
    h = h + rmsnorm(gate * (p_emb @ ple_w_proj), ple_post_g)
    return h, new_conv, v_cur


def setup_inputs(seed: int = 0) -> dict:
    key = jax.random.key(seed)
    ks = iter(jax.random.split(key, 40))
    f32 = jnp.float32

    def nrm(shape, scale):
        return jax.random.normal(next(ks), shape, f32) * scale

    def gain(shape):
        return 1.0 + 0.05 * jax.random.normal(next(ks), shape, f32)

    d = {}
    d["x_prompt"] = nrm((BATCH, SEQ, D_MODEL), 1.0)
    d["x_sample"] = nrm((DEC_BATCH, DEC_SEQ, D_MODEL), 1.0)
    d["p_prompt"] = nrm((DEPTH, BATCH, SEQ, PLE_DIM), 1.0)
    d["p_sample"] = nrm((DEPTH, DEC_BATCH, DEC_SEQ, PLE_DIM), 1.0)
    d["state_conv"] = nrm((DEPTH, DEC_BATCH, CONV_K - 1, CONV_DIM), 1.0)
    d["ffn1_pre_g"] = gain((DEPTH, D_MODEL))
    d["ffn1_post_g"] = gain((DEPTH, D_MODEL))
    d["ffn1_w_gate"] = nrm((DEPTH, D_MODEL, D_FF), D_MODEL ** -0.5)
    d["ffn1_w_up"] = nrm((DEPTH, D_MODEL, D_FF), D_MODEL ** -0.5)
    d["ffn1_w_down"] = nrm((DEPTH, D_FF, D_MODEL), D_FF ** -0.5)
    d["mix_pre_g"] = gain((DEPTH, D_MODEL))
    d["mix_post_g"] = gain((DEPTH, D_MODEL))
    d["w_in"] = nrm((DEPTH, D_MODEL, IN_COLS), D_MODEL ** -0.5)
    d["conv_w"] = nrm((DEPTH, CONV_K, CONV_DIM), CONV_K ** -0.5)
    d["v_norm_g"] = gain((DEPTH, HEADS_B, HEAD_DIM_B))
    d["w_s"] = nrm((DEPTH, HEADS_B, CHUNK, CHUNK), CHUNK ** -0.5)
    d["b_s"] = 1.0 + nrm((DEPTH, HEADS_B, CHUNK), 0.1)
    d["out_g_a"] = gain((DEPTH, CONV_DIM))
    d["out_g_b"] = gain((DEPTH, CHUNK_DIM))
    d["w_out"] = nrm((DEPTH, MIX_WIDTH, D_MODEL), MIX_WIDTH ** -0.5)
    d["ffn2_pre_g"] = gain((DEPTH, D_MODEL))
    d["ffn2_post_g"] = gain((DEPTH, D_MODEL))
    d["ffn2_w_gate"] = nrm((DEPTH, D_MODEL, D_FF), D_MODEL ** -0.5)
    d["ffn2_w_up"] = nrm((DEPTH, D_MODEL, D_FF), D_MODEL ** -0.5)
    d["ffn2_w_down"] = nrm((DEPTH, D_FF, D_MODEL), D_FF ** -0.5)
    d["ple_w_gate"] = nrm((DEPTH, D_MODEL, D_MODEL), D_MODEL ** -0.5)
    d["ple_w_proj"] = nrm((DEPTH, PLE_DIM, D_MODEL), PLE_DIM ** -0.5)
    d["ple_post_g"] = gain((DEPTH, D_MODEL))
    return d


def reference(x_prompt, x_sample, p_prompt, p_sample, state_conv,
              ffn1_pre_g, ffn1_post_g, ffn1_w_gate, ffn1_w_up, ffn1_w_down,
              mix_pre_g, mix_post_g, w_in, conv_w, v_norm_g, w_s, b_s, out_g_a, out_g_b, w_out,
              ffn2_pre_g, ffn2_post_g, ffn2_w_gate, ffn2_w_up, ffn2_w_down,
              ple_w_gate, ple_w_proj, ple_post_g):
    hp, hs = x_prompt, x_sample
    conv_p_list, conv_s_list, vp_list, vs_list = [], [], [], []
    for i in range(DEPTH):
        w = (ffn1_pre_g[i], ffn1_post_g[i], ffn1_w_gate[i], ffn1_w_up[i], ffn1_w_down[i],
             mix_pre_g[i], mix_post_g[i], w_in[i], conv_w[i], v_norm_g[i], w_s[i], b_s[i],
             out_g_a[i], out_g_b[i], w_out[i],
             ffn2_pre_g[i], ffn2_post_g[i], ffn2_w_gate[i], ffn2_w_up[i], ffn2_w_down[i],
             ple_w_gate[i], ple_w_proj[i], ple_post_g[i])
        zeros_prev = jnp.zeros((hp.shape[0], CONV_K - 1, CONV_DIM), hp.dtype)
        hp, conv_p, v_p = layer(hp, p_prompt[i], zeros_prev, *w)
        hs, conv_s, v_s = layer(hs, p_sample[i], state_conv[i].astype(hs.dtype), *w)
        conv_p_list.append(conv_p)
        conv_s_list.append(conv_s)
        vp_list.append(v_p)
        vs_list.append(v_s)
    new_conv_prompt = jnp.stack(conv_p_list)
    new_conv_sample = jnp.stack(conv_s_list)
    chunk_v_prompt = jnp.stack(vp_list)
    chunk_v_sample = jnp.stack(vs_list)
    return (hp, hs, new_conv_prompt, new_conv_sample, chunk_v_prompt, chunk_v_sample)
```

```python
import numpy as np
import concourse.bass as bass
import concourse.mybir as mybir
from concourse.bass_utils import run_bass_kernel_spmd

F32 = mybir.dt.float32
BF16 = mybir.dt.bfloat16
AF = mybir.ActivationFunctionType
ALU = mybir.AluOpType

NCORES = 8
D = 1024
KC = 8
FF = 2816
NFF = 22
SEQ = 2048
NSAMP = 128
NTOK = SEQ + NSAMP
SMAX = 1152
PLE = 256
EPS = 1e-6
NSLOT = 5
SLOT_EL = 4096

G1PRE, G1POST, GMPRE, GMPOST, GCW, GV, GA, GB, G2PRE, G2POST, GPLE = 0, 8, 16, 24, 32, 44, 48, 52, 56, 64, 72
NGROWS = 80
H1POST, H2POST = 80, 88
NGCOLS = 96


_STOP = None


class _Stop(Exception):
    pass


def _chk(tag):
    if _STOP is not None and _STOP == tag:
        raise _Stop()


class Ev:
    __slots__ = ("sem", "val", "key", "idx")

    def __init__(self, sem, val, key, idx=None):
        self.sem, self.val, self.key, self.idx = sem, val, key, idx


_COLLECT = None
_USED = None


def _flat(evs, out):
    for e in evs:
        if e is None:
            continue
        if isinstance(e, (list, tuple)):
            _flat(e, out)
        else:
            out.append(e)
    return out


class Eng:
    def __init__(self, nc, e, name):
        self.e, self.name = e, name
        self.sem = nc.alloc_semaphore("sem_" + name)
        self.n = 0
        self.idx = 0
        self.seen = {}

    def wait(self, *evs):
        best = {}
        for ev in _flat(evs, []):
            if ev.key not in best or best[ev.key].val < ev.val:
                best[ev.key] = ev
        for ev in best.values():
            if self.seen.get(ev.key, 0) >= ev.val:
                continue
            self.e.wait_ge(ev.sem, ev.val)
            self.seen[ev.key] = ev.val
            if _COLLECT is not None and ev.idx is not None:
                _COLLECT.add((ev.key, ev.idx))

    def mark(self, ins):
        self.idx += 1
        if _USED is not None and (self.name, self.idx) not in _USED:
            return Ev(self.sem, self.n, self.name, self.idx)
        self.n += 1
        ins.then_inc(self.sem, 1)
        return Ev(self.sem, self.n, self.name, self.idx)

    def now(self):
        if _COLLECT is not None:
            _COLLECT.add((self.name, self.idx))
        return Ev(self.sem, self.n, self.name, self.idx)


class DmaSem:
    def __init__(self, nc, name):
        self.sem = nc.alloc_semaphore(name)
        self.name = name
        self.n = 0

    def add(self, ins):
        self.n += 16
        ins.then_inc(self.sem, 16)
        return Ev(self.sem, self.n, self.name)


class Bank:
    def __init__(self, t):
        self.t = t
        self.free = []


def build_program():
    nc = bass.Bass("TRN2", target_bir_lowering=False)

    def din(name, shape):
        return nc.dram_tensor(name, list(shape), F32, kind="ExternalInput").ap()

    def dout(name, shape):
        return nc.dram_tensor(name, list(shape), F32, kind="ExternalOutput").ap()

    x_d = din("x", [NTOK, D])
    p_d = din("p", [NTOK, PLE])
    sc_d = din("sc", [32, 512])
    grows_d = din("grows", [NGROWS, 128])
    w1g_d = din("w1g", [D, FF]); w1u_d = din("w1u", [D, FF]); w1d_d = din("w1d", [FF, D])
    w2g_d = din("w2g", [D, FF]); w2u_d = din("w2u", [D, FF]); w2d_d = din("w2d", [FF, D])
    win_d = din("win", [D, 2560])
    wout_d = din("wout", [D, D])
    wpg_d = din("wpg", [D, D])
    wpp_d = din("wpp", [PLE, D])
    ws_d = din("ws", [8, 128, 128])
    bs_d = din("bs", [8, 128])

    y_d = dout("y", [NTOK, D])
    ncp_d = dout("ncp", [2, 512])
    ncs_d = dout("ncs", [32, 512])
    cvp_d = dout("cvp", [128, 512])
    cvs_d = dout("cvs", [128, 512])

    SP = Eng(nc, nc.sync, "sp")
    PE = Eng(nc, nc.tensor, "pe")
    ACT = Eng(nc, nc.scalar, "act")
    DVE = Eng(nc, nc.vector, "dve")
    POOL = Eng(nc, nc.gpsimd, "pool")

    h_t = nc.alloc_sbuf_tensor("h", [128, KC, SMAX], F32)
    yn_t = nc.alloc_sbuf_tensor("yn", [128, 9216], F32)
    big_t = nc.alloc_sbuf_tensor("big", [128, NFF * SMAX], BF16)
    ring_t = nc.alloc_sbuf_tensor("ring", [128, NSLOT, SLOT_EL], BF16)
    pT_t = nc.alloc_sbuf_tensor("pT", [128, 2, SMAX], BF16)
    sq_t = nc.alloc_sbuf_tensor("sq", [128, 4, 512], BF16)
    rs_t = nc.alloc_sbuf_tensor("rs", [128, 4, 512], F32)
    sg_t = nc.alloc_sbuf_tensor("sg", [128, 2, 512], F32)
    xs_t = nc.alloc_sbuf_tensor("xs", [128, 2, D], F32)
    pst_t = nc.alloc_sbuf_tensor("pst", [128, 2, PLE], F32)
    vnT_t = nc.alloc_sbuf_tensor("vnT", [128, 2, 512], BF16)
    ident_t = nc.alloc_sbuf_tensor("ident", [128, 128], F32)
    ones_t = nc.alloc_sbuf_tensor("ones", [128, 128], BF16)
    bones_t = nc.alloc_sbuf_tensor("bones", [128, 128], BF16)
    wmT_t = nc.alloc_sbuf_tensor("wmT", [128, 8, 128], BF16)
    wmTs_t = nc.alloc_sbuf_tensor("wmTs", [128, 8, 128], BF16)
    bbc_t = nc.alloc_sbuf_tensor("bbc", [128, 4, 128], F32)
    bbcs_t = nc.alloc_sbuf_tensor("bbcs", [128, 4, 128], F32)
    gT_t = nc.alloc_sbuf_tensor("gT", [128, NGCOLS], F32)
    mask_t = nc.alloc_sbuf_tensor("mask", [128, 128], F32)
    mbd_t = nc.alloc_sbuf_tensor("mbd", [128, 128], F32)
    R_t = nc.alloc_sbuf_tensor("R", [8, 128], F32)
    carry_t = nc.alloc_sbuf_tensor("carry", [128, 4, 2], F32)
    ncv_t = nc.alloc_sbuf_tensor("ncv", [128, 4, 32], F32)

    bigf = big_t[:].bitcast(F32)
    ynb = yn_t[:].bitcast(BF16)
    hmid = big_t[:].rearrange("p (j s) -> p j s", j=NFF)
    zp = bigf[:, 0:4104].rearrange("p (c s) -> p c s", c=4)
    zs = bigf[:, 4104:4744].rearrange("p (c q s) -> p c q s", c=4, q=16)
    yab = bigf[:, 4744:4744 + 4608].rearrange("p (c s) -> p c s", c=4)
    ynm = big_t[:, 0:9216].rearrange("p (c s) -> p c s", c=8)
    gbuf = bigf[:, 0:9216].rearrange("p (c s) -> p c s", c=8)
    grows_v = bigf[:, 12416:12544]
    wst_v = bigf[:, 128:128 + 1024].rearrange("p (h s) -> p h s", h=8)
    T1_v = bigf[:, 1152:1152 + 1024]
    scst_v = bigf[:, 2176:2176 + 512]
    wmf_v = bigf[:, 2688:2688 + 1024].rearrange("p (h s) -> p h s", h=8)
    vo_v = bigf[:, 10432:10432 + 512]
    nco_v = bigf[:, 9400:9400 + 1024].rearrange("p (a s) -> p a s", a=2)

    ring_banks = [Bank(nc.alloc_psum_tensor("pb%d" % i, [128, 512], F32)) for i in range(5)]
    stat_banks = [Bank(nc.alloc_psum_tensor("ps%d" % i, [128, 512], F32)) for i in range(3)]
    st = {"rb": 0, "sb": 0, "sq": 0, "rs": 0, "sg": 0}

    pending = []

    def flush():
        while pending:
            pending.pop(0)()

    def get_bank():
        b = ring_banks[st["rb"] % 5]
        st["rb"] += 1
        PE.wait(b.free)
        b.free = []
        return b

    def get_stat():
        b = stat_banks[st["sb"] % 3]
        st["sb"] += 1
        return b

    def mm_group(out_ap, pairs, wait=None):
        PE.wait(wait)
        n = len(pairs)
        ins = None
        for i, (l, r) in enumerate(pairs):
            ins = nc.tensor.matmul(out_ap, l, r, start=(i == 0), stop=(i == n - 1))
        ev = PE.mark(ins)
        flush()
        return ev

    def transposes(out_aps, in_aps, wait=None):
        PE.wait(wait)
        ins = None
        for o, i_ in zip(out_aps, in_aps):
            ins = nc.tensor.transpose(o, i_, ident_t[0:i_.shape[0], 0:i_.shape[0]])
        return PE.mark(ins)

    sq_free = [None] * 4

    def sq_chunk():
        i = st["sq"] % 4
        st["sq"] += 1
        return i

    rs_free = [None] * 4

    def rs_buf():
        i = st["rs"] % 4
        st["rs"] += 1
        return i

    def act(wait, **kw):
        ACT.wait(wait)
        return ACT.mark(nc.scalar.activation(**kw))

    slot_sem = [DmaSem(nc, "slot%d" % i) for i in range(NSLOT)]
    slot_rel = [None] * NSLOT
    fills = []
    fill_ev = {}
    wr = {"issued": 0}

    MAXFLY = 2

    def issue_next():
        f = wr["issued"]
        if f >= len(fills):
            return
        s = f % NSLOT
        POOL.wait(slot_rel[s], fill_ev.get(f - MAXFLY))
        ev = None
        for ins in fills[f](ring_t[:, s, :]):
            ev = slot_sem[s].add(ins)
        fill_ev[f] = ev
        wr["issued"] += 1

    def release(f, ev):
        slot_rel[f % NSLOT] = ev
        issue_next()

    def wdma(out, in_):
        return nc.gpsimd.dma_start(out=out, in_=in_)

    def kview(w_d):
        return w_d.rearrange("(k p) n -> p k n", p=128)

    def fill_gu(wg, wu, j0):
        def f(sv):
            g = sv[:, 0:2048].rearrange("p (k n) -> p k n", k=8)
            u = sv[:, 2048:4096].rearrange("p (k n) -> p k n", k=8)
            return [wdma(g, kview(wg)[:, :, j0 * 128:(j0 + 2) * 128]),
                    wdma(u, kview(wu)[:, :, j0 * 128:(j0 + 2) * 128])]
        return f

    def fill_cols(w_d, nk, c0, ncol):
        def f(sv):
            v = sv[:, 0:nk * ncol].rearrange("p (k n) -> p k n", k=nk)
            if nk > 8:
                hk = nk // 2
                return [wdma(v[:, 0:hk, :], kview(w_d)[:, 0:hk, c0:c0 + ncol]),
                        wdma(v[:, hk:nk, :], kview(w_d)[:, hk:nk, c0:c0 + ncol])]
            return [wdma(v, kview(w_d)[:, :, c0:c0 + ncol])]
        return f

    IN_ORDER = [1, 2, 4, 0, 3]
    for _st in range(2):
        ntl = 2
        for wset in ((w1g_d, w1u_d, w1d_d), None, (w2g_d, w2u_d, w2d_d)):
            if wset is None:
                for blk in IN_ORDER:
                    fills.append(fill_cols(win_d, 8, blk * 512, 512))
                for _t in range(ntl):
                    fills.append(fill_cols(wout_d, 8, 0, 512))
                    fills.append(fill_cols(wout_d, 8, 512, 512))
                continue
            wg, wu, wd = wset
            for j0 in range(0, NFF, 2):
                fills.append(fill_gu(wg, wu, j0))
            for _t in range(ntl):
                for m in range(8):
                    fills.append(fill_cols(wd, NFF, m * 128, 128))
        for _t in range(ntl):
            fills.append(fill_cols(wpg_d, 8, 0, 512))
            fills.append(fill_cols(wpg_d, 8, 512, 512))
            fills.append(fill_cols(wpp_d, 2, 0, 1024))
    fcur = {"i": 0}

    def next_fill():
        f = fcur["i"]
        fcur["i"] += 1
        return f, fill_ev[f]

    ld_g = DmaSem(nc, "ld_gain")
    ld = DmaSem(nc, "ld_const")
    pc = {"ev": None}

    def pchain(ins):
        pc["ev"] = POOL.mark(ins)
        POOL.wait(pc["ev"])
        return pc["ev"]

    ev_gz = pchain(nc.gpsimd.memset(grows_v, 0.0))
    SP.wait(ev_gz)
    ev_ldg = ld_g.add(nc.sync.dma_start(out=grows_v[0:NGROWS, :], in_=grows_d))
    pchain(nc.gpsimd.memset(ident_t[:], 0.0))
    pchain(nc.gpsimd.affine_select(out=ident_t[:], in_=ident_t[:], compare_op=ALU.not_equal, fill=1.0, base=0,
                                   pattern=[[-1, 128]], channel_multiplier=1))
    pchain(nc.gpsimd.memset(ones_t[:], 1.0 / 1024.0))
    pchain(nc.gpsimd.memset(bones_t[:], 0.0))
    pchain(nc.gpsimd.memset(bones_t[0:64, 0:64], 1.0 / 64.0))
    ev_pc0 = pchain(nc.gpsimd.memset(bones_t[64:128, 64:128], 1.0 / 64.0))
    for _ in range(NSLOT):
        issue_next()
    pchain(nc.gpsimd.memset(mask_t[:], 1.0))
    pchain(nc.gpsimd.affine_select(out=mask_t[:], in_=mask_t[:], compare_op=ALU.is_ge, fill=0.0, base=0,
                                   pattern=[[1, 128]], channel_multiplier=-1))
    mbd3 = mbd_t[:].rearrange("p (q s) -> p q s", q=16)
    pchain(nc.gpsimd.memset(mbd_t[:], 1.0))
    pchain(nc.gpsimd.affine_select(out=mbd3, in_=mbd3, compare_op=ALU.is_ge, fill=0.0, base=0,
                                   pattern=[[-8, 16], [0, 8]], channel_multiplier=1))
    pchain(nc.gpsimd.affine_select(out=mbd3, in_=mbd3, compare_op=ALU.is_ge, fill=0.0, base=7,
                                   pattern=[[8, 16], [0, 8]], channel_multiplier=-1))
    R3 = R_t[:].rearrange("p (q s) -> p q s", q=16)
    pchain(nc.gpsimd.memset(R_t[:], 0.0))
    pchain(nc.gpsimd.affine_select(out=R3, in_=R3, compare_op=ALU.not_equal, fill=1.0, base=0,
                                   pattern=[[0, 16], [1, 8]], channel_multiplier=-1))
    ev_pc = pchain(nc.gpsimd.memset(carry_t[:], 0.0))

    b0 = get_bank()
    ev = transposes([b0.t[:, 0:128]], [grows_v], wait=[ev_ldg, ev_pc0])
    DVE.wait(ev)
    ev_g = DVE.mark(nc.vector.tensor_copy(gT_t[:, 0:NGROWS], b0.t[:, 0:NGROWS]))
    DVE.wait(ev_g)
    nc.vector.tensor_scalar(out=gT_t[:, H1POST:H1POST + 8], in0=gT_t[:, G1POST:G1POST + 8], scalar1=0.5,
                            scalar2=None, op0=ALU.mult)
    ev_g = DVE.mark(nc.vector.tensor_scalar(out=gT_t[:, H2POST:H2POST + 8], in0=gT_t[:, G2POST:G2POST + 8],
                                            scalar1=0.5, scalar2=None, op0=ALU.mult))
    b0.free = [ev_g]
    ACT.wait(ev_g, ev_pc0)
    DVE.wait(ev_g, ev_pc0)

    def gcol(i):
        return gT_t[:, i:i + 1]

    sct_t = nc.alloc_sbuf_tensor("sct", [128, 4, 32], F32)
    late = {}

    wst_s = xs_t[:, 0, :].rearrange("p (h s) -> p h s", h=8)
    wmf_s = xs_t[:, 1, :].rearrange("p (h s) -> p h s", h=8)
    T1_s = sg_t[:].rearrange("p a s -> p (a s)")
    scst_s = pst_t[:].rearrange("p a s -> p (a s)")

    def late_loads():
        ld.add(nc.sync.dma_start(out=wst_s, in_=ws_d.rearrange("h t s -> t h s")))
        ld.add(nc.sync.dma_start(out=scst_s[0:32, :], in_=sc_d))
        for hh in range(8):
            c, half = hh // 2, hh % 2
            ld.add(nc.sync.dma_start(out=bbc_t[64 * half:64 * half + 64, c, :],
                                     in_=bs_d[hh:hh + 1, :].broadcast_to([64, 128])))
            ld.add(nc.sync.dma_start(
                out=bbcs_t[64 * half:64 * half + 64, c, :].rearrange("p (q s) -> p q s", q=16),
                in_=bs_d[hh:hh + 1, 0:8].unsqueeze(1).broadcast_to([64, 16, 8])))

    def late_stage1():
        ev_ld = Ev(ld.sem, ld.n, ld.name)
        DVE.wait(ev_pc)
        ACT.wait(ev_pc)
        ev_w = None
        evt_w = None
        for half in range(2):
            b = get_bank()
            evt_w = transposes([b.t[:, k * 128:(k + 1) * 128] for k in range(4)],
                               [wst_s[:, half * 4 + k, :] for k in range(4)], wait=[ev_ld])
            DVE.wait(evt_w)
            ev_w = DVE.mark(nc.vector.tensor_tensor(
                out=wmf_s[:, half * 4:half * 4 + 4, :], in0=b.t[:].rearrange("p (h s) -> p h s", h=4),
                in1=mask_t[:].unsqueeze(1).broadcast_to([128, 4, 128]), op=ALU.mult))
            b.free = [ev_w]
        DVE.wait(ev_w)
        ev_wm = DVE.mark(nc.vector.tensor_copy(wmT_t[:], wmf_s))
        late["wm"] = [ev_wm, ev_ld]
        late["wmf"] = ev_w
        b = get_bank()
        evt_s = transposes([b.t[:, c * 32:(c + 1) * 32] for c in range(4)],
                           [scst_s[0:32, c * 128:(c + 1) * 128] for c in range(4)], wait=[ev_ld])
        DVE.wait(evt_s)
        late["sct"] = DVE.mark(nc.vector.tensor_copy(sct_t[:], b.t[:, 0:128].rearrange("p (c s) -> p c s", c=4)))
        b.free = [late["sct"]]
        xs_free[0] = [evt_w]
        for sb in p_stage1:
            sb.free = evt_s

    def late_stage2():
        DVE.wait(sg_free, late["wmf"])
        T14 = T1_s[0:8, :].rearrange("p (h q s) -> p h q s", h=8, q=16)
        ev_t1 = DVE.mark(nc.vector.tensor_copy(T14, wmf_s[0:8, :, 0:8].unsqueeze(2).broadcast_to([8, 8, 16, 8])))
        ev_mm = None
        ev_w = None
        for half in range(2):
            b = get_bank()
            PE.wait(ev_t1, ev_pc)
            ev_mm = PE.mark(nc.tensor.matmul(b.t[:], R_t[:], T1_s[0:8, half * 512:(half + 1) * 512],
                                             start=True, stop=True))
            DVE.wait(ev_mm)
            ev_w = DVE.mark(nc.vector.tensor_tensor(
                out=wmTs_t[:, half * 4:half * 4 + 4, :], in0=b.t[:].rearrange("p (h s) -> p h s", h=4),
                in1=mbd_t[:].unsqueeze(1).broadcast_to([128, 4, 128]), op=ALU.mult))
            b.free = [ev_w]
        late["wms"] = [ev_w]
        xs_free[1] = [ev_t1]
        sg_free[0] = ev_mm
        sg_free[1] = ev_mm

    out_sem = DmaSem(nc, "out")
    st_sem = [DmaSem(nc, "st0"), DmaSem(nc, "st1")]
    vo_sem = DmaSem(nc, "vost")
    class Stage:
        def __init__(self, ap, name):
            self.ap = ap
            self.sem = DmaSem(nc, name)
            self.free = None

    x_stage0 = [Stage(bigf[:, k * 1024:(k + 1) * 1024], "xs0_%d" % k) for k in range(8)]
    p_stage0 = [Stage(bigf[:, 8192 + k * 256:8192 + (k + 1) * 256], "ps0_%d" % k) for k in range(8)]
    x_stage1 = [Stage(bigf[:, 9216 + k * 1024:9216 + (k + 1) * 1024], "xs1_%d" % k) for k in range(3)]
    p_stage1 = [Stage(pst_t[:, k, :], "ps1_%d" % k) for k in range(2)]
    stg = {"x": x_stage0, "p": p_stage0, "xi": 0, "pi": 0}
    xs_free = [None, None]
    ps_free = [None, None]
    big_free = {"ev": [ev_g]}
    cnt = {"blk": 0, "vn": 0}
    vnT_free = [None, None]
    vo_free = {"ev": None}

    def rstd_from(bank, w, ev_stat):
        a = rs_buf()
        ev_s = act([ev_stat, rs_free[a]], out=rs_t[:, a, 0:w], in_=bank.t[:, 0:w], func=AF.Ln, bias=EPS, scale=1.0)
        bank.free = [ev_s]
        r = rs_buf()
        ev_r = act([ev_s, rs_free[r]], out=rs_t[:, r, 0:w], in_=rs_t[:, a, 0:w], func=AF.Exp, scale=-0.5)
        rs_free[a] = ev_r
        return r, ev_r

    def stat_mm(bank, w, lhs, sqi, first, last, ev_in):
        def f():
            PE.wait(ev_in)
            if first:
                PE.wait(bank.free)
                bank.free = []
            ins = nc.tensor.matmul(bank.t[:, 0:w], lhs, sq_t[:, sqi, 0:w], start=first, stop=last)
            sq_free[sqi] = PE.mark(ins)
            if last:
                bank.done = sq_free[sqi]
        pending.append(f)

    def square_to_stat(src_ap, w, bank, first, last, ev_in, lhs=None, defer=True):
        i = sq_chunk()
        ev_q = act([ev_in, sq_free[i]], out=sq_t[:, i, 0:w], in_=src_ap, func=AF.Square)
        stat_mm(bank, w, ones_t[:] if lhs is None else lhs, i, first, last, ev_q)
        if not defer:
            flush()
        return ev_q

    class Tile:
        def __init__(self, idx, c0, w, g0, sample):
            self.idx, self.c0, self.w, self.g0, self.sample = idx, c0, w, g0, sample
            self.off = idx * 4096
            self.cols = slice(c0, c0 + w)
            self.h_ev = None
            self.n_ev = None

        def y(self, m):
            return yn_t[:, self.off + m * self.w: self.off + (m + 1) * self.w]

        def n(self, k):
            o = 2 * self.off
            return ynb[:, o + k * self.w: o + (k + 1) * self.w]

        def n_all(self):
            o = 2 * self.off
            return ynb[:, o: o + 8 * self.w].rearrange("p (k s) -> p k s", k=8)

        def vb(self, c):
            o = self.off + 4 * self.w
            return yn_t[:, o + c * self.w: o + (c + 1) * self.w]

        def vb_all(self):
            o = self.off + 4 * self.w
            return yn_t[:, o: o + 4 * self.w].rearrange("p (c s) -> p c s", c=4)

    def prenorm(t, gidx, split=False):
        hc = getattr(t, "h_chunk", None)
        evq = [act([hc[m] if hc else t.h_ev, t.n_ev], out=t.n(m), in_=h_t[:, m, t.cols], func=AF.Square)
               for m in range(8)]

        def rest():
            bank = get_stat()
            flush()
            PE.wait(bank.free)
            bank.free = []
            ins = None
            for m in range(8):
                PE.wait(evq[m])
                ins = nc.tensor.matmul(bank.t[:, 0:t.w], ones_t[:], t.n(m), start=(m == 0), stop=(m == 7))
            bank.done = PE.mark(ins)
            r, ev_r = rstd_from(bank, t.w, bank.done)
            ev = None
            for m in range(8):
                DVE.wait(ev_r, t.h_ev)
                ev = DVE.mark(nc.vector.scalar_tensor_tensor(
                    out=t.n(m), in0=h_t[:, m, t.cols], scalar=gcol(gidx + m), in1=rs_t[:, r, 0:t.w],
                    op0=ALU.mult, op1=ALU.mult))
            rs_free[r] = ev
            t.n_ev = ev

        if split:
            return rest
        rest()
        return None

    def postnorm_residual(t, bank, gidx, src):
        r, ev_r = rstd_from(bank, t.w, bank.done)
        ev = None
        ev1 = [None] * 8

        def add(m):
            DVE.wait(ev1[m])
            return DVE.mark(nc.vector.tensor_tensor(out=h_t[:, m, t.cols], in0=h_t[:, m, t.cols], in1=src(m),
                                                    op=ALU.add))

        t.h_chunk = {}
        for m in range(8):
            DVE.wait(ev_r, t.y_ev)
            ev1[m] = DVE.mark(nc.vector.scalar_tensor_tensor(
                out=src(m), in0=src(m), scalar=gcol(gidx + m), in1=rs_t[:, r, 0:t.w],
                op0=ALU.mult, op1=ALU.mult))
            if m >= 1:
                ev = add(m - 1)
                t.h_chunk[m - 1] = ev
        ev = add(7)
        t.h_chunk[7] = ev
        rs_free[r] = ev
        t.h_ev = ev

    pend_pre = []

    def fill_groups(tl):
        if len(tl) == 3:
            return [[tl[0]], [tl[1], tl[2]]]
        return [[t] for t in tl]

    pend_pre2 = []

    def run_pend_pre():
        while pend_pre:
            r_ = pend_pre.pop(0)()
            if r_ is not None:
                pend_pre2.append(r_)

    def run_pend_pre2():
        while pend_pre2:
            pend_pre2.pop(0)()

    def out_proj(tiles, nfill, ms_per_fill, nk, rhs, rhs_ev, gidx, next_pre):
        prev = None
        rest_ = None
        for grp in fill_groups(tiles):
            banks = {t.idx: get_stat() for t in grp}
            g = 0
            for fi in range(nfill):
                f, fev = next_fill()
                sv = ring_t[:, f % NSLOT, 0:nk * ms_per_fill * 128].rearrange("p (k n) -> p k n", k=nk)
                evl = None
                for mi in range(ms_per_fill):
                    m = fi * ms_per_fill + mi
                    for t in grp:
                        b = get_bank()
                        evl = mm_group(b.t[:, 0:t.w],
                                       [(sv[:, k, mi * 128:(mi + 1) * 128], rhs(t, k)) for k in range(nk)],
                                       wait=[fev, rhs_ev(t)])
                        ev_c = act([evl], out=t.y(m), in_=b.t[:, 0:t.w], func=AF.Copy)
                        ev_q = square_to_stat(b.t[:, 0:t.w], t.w, banks[t.idx], m == 0, m == 7, evl)
                        b.free = [ev_q]
                        t.y_ev = ev_c
                    g += 1
                    if prev is not None and g == (4 if nk > 8 else 7):
                        rest_ = next_pre(prev, True)
                        prev = None
                    elif rest_ is not None and nk > 8 and g == 5:
                        rest_()
                        rest_ = None
                release(f, evl)
            flush()
            if rest_ is not None:
                rest_()
                rest_ = None
            if prev is not None:
                next_pre(prev)
                prev = None
            for t in grp[:-1]:
                postnorm_residual(t, banks[t.idx], gidx, t.y)
                pend_pre.append(lambda last=t: next_pre(last, True))
            t = grp[-1]
            postnorm_residual(t, banks[t.idx], gidx, t.y)
            prev = t
        pend_pre.append(lambda last=prev: next_pre(last, True))

    KSKEW = 3

    def ffn(tiles, g_post_half, next_pre, before_b=None):
        hm = {"ev": None}

        def gu(f, fev, j0, t):
            sv = ring_t[:, f % NSLOT, :]
            wg = sv[:, 0:2048].rearrange("p (k n) -> p k n", k=8)
            wu = sv[:, 2048:4096].rearrange("p (k n) -> p k n", k=8)
            evl = None
            for jj in range(2):
                j = j0 + jj
                bg = get_bank()
                evg = mm_group(bg.t[:, 0:t.w], [(wg[:, k, jj * 128:(jj + 1) * 128], t.n(k)) for k in range(8)],
                               wait=[fev, t.n_ev])
                bu = get_bank()
                evl = mm_group(bu.t[:, 0:t.w], [(wu[:, k, jj * 128:(jj + 1) * 128], t.n(k)) for k in range(8)])
                si = st["sg"] % 2
                st["sg"] += 1
                ev_s = act([evg, sg_free[si]], out=sg_t[:, si, 0:t.w], in_=bg.t[:, 0:t.w], func=AF.Silu)
                bg.free = [ev_s]
                DVE.wait(ev_s, evl, big_free["ev"])
                hm["ev"] = DVE.mark(nc.vector.tensor_tensor(out=hmid[:, j, t.cols], in0=bu.t[:, 0:t.w],
                                                            in1=sg_t[:, si, 0:t.w], op=ALU.mult))
                bu.free = [hm["ev"]]
                sg_free[si] = hm["ev"]
            return evl

        slots = list(range(0, NFF, 2))
        head = [next_fill() for _ in range(KSKEW)]
        evl = None
        for ti, t in enumerate(tiles):
            for si, ((f, fev), j0) in enumerate(zip(head, slots[:KSKEW])):
                evl = gu(f, fev, j0, t)
                if ti == 0 and si == 0:
                    run_pend_pre()
                if ti == 0 and si == 1:
                    run_pend_pre()
                    run_pend_pre2()
        for (f, fev) in head:
            release(f, evl)
        for j0 in slots[KSKEW:]:
            f, fev = next_fill()
            for t in tiles:
                evl = gu(f, fev, j0, t)
            release(f, evl)
        if before_b is not None:
            before_b()
        out_proj(tiles, 8, 1, NFF, lambda t, j: hmid[:, j, t.cols], lambda t: hm["ev"], g_post_half, next_pre)
        big_free["ev"] = [PE.now()]

    sg_free = [None, None]

    all_tiles = [[Tile(0, 0, 512, 0, False), Tile(1, 512, 512, 512, False)],
                 [Tile(0, 0, 512, 1024, False), Tile(1, 512, 512, 1536, False), Tile(2, 1024, 128, 2048, True)]]

    def p0_tile(t, defer_pre=False):
        h_evs = []
        for bi in range(t.c0 // 128, (t.c0 + t.w) // 128):
            xsb = stg["x"][stg["xi"] % len(stg["x"])]
            stg["xi"] += 1
            psb = stg["p"][stg["pi"] % len(stg["p"])]
            stg["pi"] += 1
            r0 = t.g0 + bi * 128 - t.c0
            SP.wait(xsb.free, psb.free, stg.get("gate"))
            evx = xsb.sem.add(nc.sync.dma_start(out=xsb.ap, in_=x_d[r0:r0 + 128, :]))
            evp = psb.sem.add(nc.sync.dma_start(out=psb.ap, in_=p_d[r0:r0 + 128, :]))
            cs = slice(bi * 128, (bi + 1) * 128)
            evt = None
            for half in range(2):
                b = get_bank()
                evt = transposes([b.t[:, k * 128:(k + 1) * 128] for k in range(4)],
                                 [xsb.ap[:, (half * 4 + k) * 128:(half * 4 + k + 1) * 128] for k in range(4)],
                                 wait=[evx])
                src = b.t[:].rearrange("p (k s) -> p k s", k=4)
                if half == 0:
                    DVE.wait(evt)
                    e2 = DVE.mark(nc.vector.tensor_copy(h_t[:, 0:4, cs], src))
                else:
                    e2 = act([evt], out=h_t[:, 4:8, cs], in_=src, func=AF.Copy)
                b.free = [e2]
                h_evs.append(e2)
            xsb.free = evt
            b = get_bank()
            evt = transposes([b.t[:, k * 128:(k + 1) * 128] for k in range(2)],
                             [psb.ap[:, k * 128:(k + 1) * 128] for k in range(2)], wait=[evp])
            e2 = act([evt], out=pT_t[:, :, cs], in_=b.t[:, 0:256].rearrange("p (k s) -> p k s", k=2), func=AF.Copy)
            b.free = [e2]
            psb.free = evt
            stg["last"] = [xsb.free, psb.free]
            t.pT_ev = e2
        t.h_ev = list(h_evs)
        t.h_chunk = None
        t.y_ev = None
        if defer_pre:
            pend_pre.append(lambda t=t: prenorm(t, G1PRE, True))
        else:
            prenorm(t, G1PRE)

    try:
        for sti in range(2):
            if _STOP is not None and _STOP == "setup":
                break
            tiles = all_tiles[sti]
            if len(tiles) == 3:
                tiles = [tiles[0], tiles[2], tiles[1]]
            next_tiles = all_tiles[sti + 1] if sti == 0 else []
            g_base = tiles[0].g0
            if sti == 0:
                for t in tiles:
                    p0_tile(t, defer_pre=(t is not tiles[0]))
                big_free["ev"] = big_free["ev"] + [sb.free for sb in x_stage0 + p_stage0]
                late_loads()

            _chk('p0%d' % sti)
            ffn(tiles, H1POST, lambda t, sp=False: prenorm(t, GMPRE, sp), late_stage1 if sti == 0 else None)

            _chk('ffn1%d' % sti)
            ev_pe_now = PE.now()
            DVE.wait(big_free["ev"])
            DVE.wait(ev_pe_now)
            ACT.wait(ev_pe_now)
            ev_init = DVE.mark(nc.vector.tensor_copy(zp[:, :, 0:2], carry_t[:]))
            if sti == 1:
                DVE.wait(late["sct"])
                ev_init = DVE.mark(nc.vector.tensor_copy(
                    zs[:, :, :, 0:2], sct_t[:].rearrange("p c (q s) -> p c q s", q=16)))

            def zview(t, c, sh):
                if t.sample:
                    return zs[:, c, :, 2 - sh:10 - sh]
                return zp[:, c, 2 + t.c0 - sh: 2 + t.c0 + t.w - sh]

            def bview(t, ap2):
                if t.sample:
                    return ap2.rearrange("p (q s) -> p q s", q=16)
                return ap2

            def wview(f):
                return ring_t[:, f % NSLOT, :].rearrange("p (k n) -> p k n", k=8)

            def proj(sv, fev, t, c):
                b = get_bank()
                ev = mm_group(b.t[:, 0:t.w], [(sv[:, k, c * 128:(c + 1) * 128], t.n(k)) for k in range(8)],
                              wait=[fev, t.n_ev])
                return b, ev

            fCA, evCA = next_fill()
            fHA, evHA = next_fill()
            z_prev = ev_init
            conv_ev = {}
            z_evs = {}
            evl = None
            for t in tiles:
                ca = []
                for c in range(4):
                    b, evl = proj(wview(fCA), evCA, t, c)
                    e = act([evl, ev_init], out=zview(t, c, 0), in_=bview(t, b.t[:, 0:t.w]), func=AF.Copy)
                    b.free = [e]
                    ca.append(e)
                run_pend_pre()
                zev = []
                for c in range(4):
                    b, evl = proj(wview(fHA), evHA, t, c)
                    DVE.wait(evl, ca[c])
                    e = DVE.mark(nc.vector.tensor_tensor(out=zview(t, c, 0), in0=bview(t, b.t[:, 0:t.w]),
                                                         in1=zview(t, c, 0), op=ALU.mult))
                    b.free = [e]
                    zev.append(e)
                run_pend_pre()
                run_pend_pre2()
                ce = [None] * 4
                for c in range(4):
                    o = bview(t, yab[:, c, t.cols])
                    DVE.wait(zev, z_prev)
                    ce[c] = DVE.mark(nc.vector.tensor_scalar(out=o, in0=zview(t, c, 0),
                                                             scalar1=gcol(GCW + 2 * 4 + c), scalar2=None, op0=ALU.mult))
                for sh, kk in ((1, 1), (2, 0)):
                    for c in range(4):
                        o = bview(t, yab[:, c, t.cols])
                        DVE.wait(ce[c])
                        ce[c] = DVE.mark(nc.vector.scalar_tensor_tensor(
                            out=o, in0=zview(t, c, sh), scalar=gcol(GCW + kk * 4 + c), in1=o,
                            op0=ALU.mult, op1=ALU.add))
                e = ce[3]
                conv_ev[t.idx] = e
                z_evs[t.idx] = zev
                z_prev = zev
            release(fCA, evl)
            release(fHA, evl)
            DVE.wait(z_evs[1], z_prev, conv_ev[1])
            ev_carry = DVE.mark(nc.vector.tensor_copy(carry_t[:], zp[:, :, 1024:1026]))
            ev_zdone = [ev_carry, conv_ev[tiles[-1].idx]]
            if sti == 1:
                ev_nv = DVE.mark(nc.vector.tensor_copy(ncv_t[:].rearrange("p c (q s) -> p c q s", q=16),
                                                       zs[:, :, :, 8:10]))
                ev_zdone.append(ev_nv)
            fV, evV = next_fill()
            fBA, evBA = next_fill()
            fU, evU = next_fill()
            for t in tiles:
                evs = []

                def finish_v(c, bank, e_c, t=t, evs=evs):
                    r, ev_r = rstd_from(bank, t.w, bank.done)
                    DVE.wait(ev_r, e_c)
                    e = DVE.mark(nc.vector.scalar_tensor_tensor(
                        out=t.vb(c), in0=t.vb(c), scalar=gcol(GV + c), in1=rs_t[:, r, 0:t.w],
                        op0=ALU.mult, op1=ALU.mult))
                    rs_free[r] = e
                    evs.append(e)

                prev_v = None
                for c in range(4):
                    b, evl = proj(wview(fV), evV, t, c)
                    if prev_v is not None:
                        finish_v(*prev_v)
                    e_c = act([evl], out=t.vb(c), in_=b.t[:, 0:t.w], func=AF.Copy)
                    bank = get_stat()
                    e_q = square_to_stat(b.t[:, 0:t.w], t.w, bank, True, True, evl, lhs=bones_t[:])
                    b.free = [e_q]
                    prev_v = (c, bank, e_c)
                bank_a = get_stat()
                ya_ev = []
                for c in range(4):
                    b, evl = proj(wview(fBA), evBA, t, c)
                    if prev_v is not None:
                        finish_v(*prev_v)
                        prev_v = None
                    DVE.wait(evl, conv_ev[t.idx])
                    e = DVE.mark(nc.vector.tensor_tensor(out=yab[:, c, t.cols], in0=b.t[:, 0:t.w],
                                                         in1=yab[:, c, t.cols], op=ALU.mult))
                    b.free = [e]
                    ya_ev.append(e)
                    square_to_stat(yab[:, c, t.cols], t.w, bank_a, c == 0, c == 3, e)
                def blk_T(kb, t=t, evs=evs):
                    bs_ = slice(kb * 128, (kb + 1) * 128)
                    b = get_bank()
                    evt = transposes([b.t[:, c * 128:(c + 1) * 128] for c in range(4)],
                                     [t.vb(c)[:, bs_] for c in range(4)], wait=evs)
                    vi = cnt["vn"] % 2
                    cnt["vn"] += 1
                    e_v = act([evt, vnT_free[vi]], out=vnT_t[:, vi, :], in_=b.t[:], func=AF.Copy)
                    frees = [e_v]
                    is_out = (sti == 1) and (t.sample or (t.idx == 1 and kb == 3))
                    if is_out:
                        DVE.wait(evt, e_v, vo_free["ev"])
                        e_o = DVE.mark(nc.vector.tensor_copy(vo_v, b.t[:]))
                        frees.append(e_o)
                        SP.wait(e_o)
                        vo_free["ev"] = vo_sem.add(nc.sync.dma_start(out=cvs_d if t.sample else cvp_d, in_=vo_v))
                    b.free = frees
                    return kb, vi, e_v

                def blk_M(kb, vi, e_v, t=t):
                    bs_ = slice(kb * 128, (kb + 1) * 128)
                    b2 = get_bank()
                    PE.wait(e_v, late["wms"] if t.sample else late["wm"])
                    wm = wmTs_t if t.sample else wmT_t
                    ins = None
                    for hh in range(8):
                        c, half = hh // 2, hh % 2
                        ins = nc.tensor.matmul(b2.t[64 * half:64 * half + 64, c * 128:(c + 1) * 128],
                                               vnT_t[:, vi, hh * 64:(hh + 1) * 64], wm[:, hh, :],
                                               start=True, stop=True)
                    evm = PE.mark(ins)
                    vnT_free[vi] = evm
                    DVE.wait(evm)
                    e = DVE.mark(nc.vector.tensor_tensor(
                        out=t.vb_all()[:, :, bs_], in0=b2.t[:].rearrange("p (c s) -> p c s", c=4),
                        in1=(bbcs_t if t.sample else bbc_t)[:], op=ALU.add))
                    b2.free = [e]
                    t.mix_ev = e

                pT = None
                for kb in range(t.w // 128):
                    cur = blk_T(kb)
                    if pT is not None:
                        blk_M(*pT)
                    pT = cur
                blk_M(*pT)
                flush()
                r, ev_r = rstd_from(bank_a, t.w, bank_a.done)
                e = None
                for c in range(4):
                    DVE.wait(ev_r, ya_ev, ev_zdone)
                    e = DVE.mark(nc.vector.scalar_tensor_tensor(
                        out=ynm[:, c, t.cols], in0=yab[:, c, t.cols], scalar=gcol(GA + c), in1=rs_t[:, r, 0:t.w],
                        op0=ALU.mult, op1=ALU.mult))
                rs_free[r] = e
                t.yna_ev = e
                bank_b = get_stat()
                yb_ev = []
                for c in range(4):
                    b, evl = proj(wview(fU), evU, t, c)
                    DVE.wait(evl, t.mix_ev)
                    e = DVE.mark(nc.vector.tensor_tensor(out=t.vb(c), in0=b.t[:, 0:t.w], in1=t.vb(c), op=ALU.mult))
                    b.free = [e]
                    yb_ev.append(e)
                    square_to_stat(t.vb(c), t.w, bank_b, c == 0, c == 3, e)
                flush()
                r, ev_r = rstd_from(bank_b, t.w, bank_b.done)
                e = None
                for c in range(4):
                    DVE.wait(ev_r, yb_ev, ev_zdone)
                    e = DVE.mark(nc.vector.scalar_tensor_tensor(
                        out=ynm[:, 4 + c, t.cols], in0=t.vb(c), scalar=gcol(GB + c), in1=rs_t[:, r, 0:t.w],
                        op0=ALU.mult, op1=ALU.mult))
                rs_free[r] = e
                t.ynm_ev = [t.yna_ev, e]
            release(fV, evl)
            release(fBA, evl)
            release(fU, evl)
            if sti == 1:
                b = get_bank()
                evt = transposes([b.t[0:2, c * 128:(c + 1) * 128] for c in range(4)],
                                 [carry_t[:, c, :] for c in range(4)], wait=[ev_carry])
                e = act([evt], out=nco_v[0:2, 0, :], in_=b.t[0:2, :], func=AF.Copy)
                b.free = [e]
                SP.wait(e)
                out_sem.add(nc.sync.dma_start(out=ncp_d, in_=nco_v[0:2, 0, :]))
                b = get_bank()
                evt = transposes([b.t[0:32, c * 128:(c + 1) * 128] for c in range(4)],
                                 [ncv_t[:, c, :] for c in range(4)], wait=[ev_nv])
                e = act([evt], out=nco_v[0:32, 1, :], in_=b.t[0:32, :], func=AF.Copy)
                b.free = [e]
                SP.wait(e)
                out_sem.add(nc.sync.dma_start(out=ncs_d, in_=nco_v[0:32, 1, :]))

            out_proj(tiles, 2, 4, 8, lambda t, k: ynm[:, k, t.cols], lambda t: t.ynm_ev, GMPOST,
                     lambda t, sp=False: prenorm(t, G2PRE, sp))
            big_free["ev"] = [PE.now()]
            if out_sem.n:
                big_free["ev"].append(Ev(out_sem.sem, out_sem.n, out_sem.name))
            if vo_sem.n:
                big_free["ev"].append(Ev(vo_sem.sem, vo_sem.n, vo_sem.name))

            _chk('mix%d' % sti)
            def ple_cast(t, sp=False):
                t.n_ev = act([t.h_ev, t.n_ev], out=t.n_all(), in_=h_t[:, :, t.cols], func=AF.Copy)
                return None

            ffn(tiles, H2POST, ple_cast, late_stage2 if sti == 0 else None)

            _chk('ffn2%d' % sti)
            ev_pe_now = PE.now()

            def out_tile(t):
                for bi in range(t.c0 // 128, (t.c0 + t.w) // 128):
                    i = cnt["blk"] % 2
                    cnt["blk"] += 1
                    r0 = g_base + bi * 128
                    cs = slice(bi * 128, (bi + 1) * 128)
                    e1 = e2 = None
                    for half in range(2):
                        b = get_bank()
                        evt = transposes([b.t[:, k * 128:(k + 1) * 128] for k in range(4)],
                                         [h_t[:, half * 4 + k, cs] for k in range(4)], wait=[t.h_ev])
                        dst = xs_t[:, i, half * 512:(half + 1) * 512]
                        if half == 0:
                            DVE.wait(evt, xs_free[i])
                            e1 = DVE.mark(nc.vector.tensor_copy(dst, b.t[:]))
                            b.free = [e1]
                        else:
                            e2 = act([evt, xs_free[i]], out=dst, in_=b.t[:], func=AF.Copy)
                            b.free = [e2]
                    ACT.wait(e1, e2)
                    xs_free[i] = st_sem[i].add(nc.scalar.dma_start(out=y_d[r0:r0 + 128, :], in_=xs_t[:, i, :]))

            ple_tiles = sorted(tiles, key=lambda t: t.idx)
            if next_tiles:
                stg.update({"x": x_stage1, "p": p_stage1, "xi": 0, "pi": 0, "gate": ev_pe_now})
            prev = []
            for grp in fill_groups(ple_tiles):
                gate_ev = {}
                for fi in range(2):
                    f, fev = next_fill()
                    sv = ring_t[:, f % NSLOT, :].rearrange("p (k n) -> p k n", k=8)
                    for mi in range(4):
                        m = fi * 4 + mi
                        for t in grp:
                            b = get_bank()
                            evl = mm_group(b.t[:, 0:t.w],
                                           [(sv[:, k, mi * 128:(mi + 1) * 128], t.n(k)) for k in range(8)],
                                           wait=[fev, t.n_ev])
                            e = act([evl, ev_pe_now], out=gbuf[:, m, t.cols], in_=b.t[:, 0:t.w], func=AF.Sigmoid)
                            b.free = [e]
                            gate_ev[(t.idx, m)] = e
                    release(f, evl)
                    if fi == 1:
                        run_pend_pre()
                f, fev = next_fill()
                sv = ring_t[:, f % NSLOT, 0:2048].rearrange("p (k n) -> p k n", k=2)
                banks = {t.idx: get_stat() for t in grp}
                for m in range(8):
                    for t in grp:
                        b = get_bank()
                        evl = mm_group(b.t[:, 0:t.w],
                                       [(sv[:, k, m * 128:(m + 1) * 128], pT_t[:, k, t.cols]) for k in range(2)],
                                       wait=[fev, t.pT_ev])
                        DVE.wait(evl, gate_ev[(t.idx, m)])
                        e = DVE.mark(nc.vector.tensor_tensor(out=gbuf[:, m, t.cols], in0=b.t[:, 0:t.w],
                                                             in1=gbuf[:, m, t.cols], op=ALU.mult))
                        b.free = [e]
                        square_to_stat(gbuf[:, m, t.cols], t.w, banks[t.idx], m == 0, m == 7, e)
                        t.y_ev = e
                release(f, evl)
                flush()
                for t in grp:
                    postnorm_residual(t, banks[t.idx], GPLE, lambda m, t=t: gbuf[:, m, t.cols])
                for pt in prev:
                    out_tile(pt)
                    if next_tiles:
                        p0_tile(next_tiles[pt.idx])
                prev = grp
            big_free["ev"] = [t.h_ev for t in tiles]
            for pt in prev:
                out_tile(pt)
            for nt_ in next_tiles[prev[-1].idx:]:
                p0_tile(nt_, defer_pre=(nt_.idx != 0))
            if next_tiles:
                big_free["ev"] = big_free["ev"] + [sb.free for sb in x_stage1]
            _chk('ple%d' % sti)

    except _Stop:
        flush()

    for ds in [out_sem, vo_sem] + st_sem:
        SP.wait(Ev(ds.sem, ds.n, ds.name))
    assert _STOP is not None or fcur["i"] == len(fills), (fcur["i"], len(fills))
    return nc


def build_two_pass():
    global _COLLECT, _USED
    _COLLECT, _USED = set(), None
    build_program()
    _USED, _COLLECT = _COLLECT, None
    try:
        return build_program()
    finally:
        _USED = None


_CACHE = {}


def kernel(**inputs):
    f32 = np.float32
    g = lambda k: np.asarray(inputs[k], dtype=f32)
    xp, xsm = g("x_prompt"), g("x_sample")
    pp, psm = g("p_prompt")[0], g("p_sample")[0]
    scv = g("state_conv")[0]
    grows = np.concatenate([
        g("ffn1_pre_g")[0].reshape(8, 128), g("ffn1_post_g")[0].reshape(8, 128),
        g("mix_pre_g")[0].reshape(8, 128), g("mix_post_g")[0].reshape(8, 128),
        g("conv_w")[0].reshape(12, 128), g("v_norm_g")[0].reshape(4, 128),
        g("out_g_a")[0].reshape(4, 128), g("out_g_b")[0].reshape(4, 128),
        g("ffn2_pre_g")[0].reshape(8, 128), g("ffn2_post_g")[0].reshape(8, 128),
        g("ple_post_g")[0].reshape(8, 128)], axis=0)
    assert grows.shape == (NGROWS, 128)
    shared = {
        "grows": np.ascontiguousarray(grows),
        "w1g": g("ffn1_w_gate")[0], "w1u": g("ffn1_w_up")[0], "w1d": g("ffn1_w_down")[0],
        "w2g": g("ffn2_w_gate")[0], "w2u": g("ffn2_w_up")[0], "w2d": g("ffn2_w_down")[0],
        "win": g("w_in")[0], "wout": g("w_out")[0], "wpg": g("ple_w_gate")[0], "wpp": g("ple_w_proj")[0],
        "ws": g("w_s")[0], "bs": g("b_s")[0],
    }
    shared = {k: np.ascontiguousarray(v) for k, v in shared.items()}
    in_maps = []
    for c in range(NCORES):
        m = dict(shared)
        m["x"] = np.ascontiguousarray(np.concatenate([xp[c], xsm[16 * c:16 * c + 16].reshape(128, D)], axis=0))
        m["p"] = np.ascontiguousarray(np.concatenate([pp[c], psm[16 * c:16 * c + 16].reshape(128, PLE)], axis=0))
        m["sc"] = np.ascontiguousarray(scv[16 * c:16 * c + 16].reshape(32, 512))
        in_maps.append(m)
    if "nc" not in _CACHE:
        _CACHE["nc"] = build_two_pass()
    res = run_bass_kernel_spmd(_CACHE["nc"], in_maps, core_ids=list(range(NCORES)))
    R = res.results
    y_prompt = np.stack([np.asarray(R[c]["y"])[:SEQ] for c in range(NCORES)]).astype(f32)
    y_sample = np.concatenate([np.asarray(R[c]["y"])[SEQ:].reshape(16, 8, D) for c in range(NCORES)]).astype(f32)
    ncp = np.stack([np.asarray(R[c]["ncp"]) for c in range(NCORES)])[None].astype(f32)
    ncs = np.concatenate([np.asarray(R[c]["ncs"]).reshape(16, 2, 512) for c in range(NCORES)])[None].astype(f32)
    cvp = np.stack([np.asarray(R[c]["cvp"]).reshape(128, 8, 64) for c in range(NCORES)])[None].astype(f32)
    cvs = np.concatenate([np.asarray(R[c]["cvs"]).reshape(16, 8, 8, 64) for c in range(NCORES)])[None].astype(f32)
    return (y_prompt, y_sample, ncp, ncs, cvp, cvs)
```

```python
import numpy as np
import concourse.bass as bass
import concourse.mybir as mybir
from concourse.bass_utils import run_bass_kernel_spmd

F32 = mybir.dt.float32
BF16 = mybir.dt.bfloat16
AF = mybir.ActivationFunctionType
ALU = mybir.AluOpType

NCORES = 8
D = 1024
KC = 8
FF = 2816
NFF = 22
SEQ = 2048
NSAMP = 128
NTOK = SEQ + NSAMP
SMAX = 1152
PLE = 256
EPS = 1e-6
NSLOT = 5
SLOT_EL = 4096

G1PRE, G1POST, GMPRE, GMPOST, GCW, GV, GA, GB, G2PRE, G2POST, GPLE = 0, 8, 16, 24, 32, 44, 48, 52, 56, 64, 72
NGROWS = 80
H1POST, H2POST = 80, 88
NGCOLS = 96


_STOP = None


class _Stop(Exception):
    pass


def _chk(tag):
    if _STOP is not None and _STOP == tag:
        raise _Stop()


class Ev:
    __slots__ = ("sem", "val", "key", "idx")

    def __init__(self, sem, val, key, idx=None):
        self.sem, self.val, self.key, self.idx = sem, val, key, idx


_COLLECT = None
_USED = None


def _flat(evs, out):
    for e in evs:
        if e is None:
            continue
        if isinstance(e, (list, tuple)):
            _flat(e, out)
        else:
            out.append(e)
    return out


class Eng:
    def __init__(self, nc, e, name):
        self.e, self.name = e, name
        self.sem = nc.alloc_semaphore("sem_" + name)
        self.n = 0
        self.idx = 0
        self.seen = {}

    def wait(self, *evs):
        best = {}
        for ev in _flat(evs, []):
            if ev.key not in best or best[ev.key].val < ev.val:
                best[ev.key] = ev
        for ev in best.values():
            if self.seen.get(ev.key, 0) >= ev.val:
                continue
            self.e.wait_ge(ev.sem, ev.val)
            self.seen[ev.key] = ev.val
            if _COLLECT is not None and ev.idx is not None:
                _COLLECT.add((ev.key, ev.idx))

    def mark(self, ins):
        self.idx += 1
        if _USED is not None and (self.name, self.idx) not in _USED:
            return Ev(self.sem, self.n, self.name, self.idx)
        self.n += 1
        ins.then_inc(self.sem, 1)
        return Ev(self.sem, self.n, self.name, self.idx)

    def now(self):
        if _COLLECT is not None:
            _COLLECT.add((self.name, self.idx))
        return Ev(self.sem, self.n, self.name, self.idx)


class DmaSem:
    def __init__(self, nc, name):
        self.sem = nc.alloc_semaphore(name)
        self.name = name
        self.n = 0

    def add(self, ins):
        self.n += 16
        ins.then_inc(self.sem, 16)
        return Ev(self.sem, self.n, self.name)


class Bank:
    def __init__(self, t):
        self.t = t
        self.free = []


def build_program():
    nc = bass.Bass("TRN2", target_bir_lowering=False)

    def din(name, shape):
        return nc.dram_tensor(name, list(shape), F32, kind="ExternalInput").ap()

    def dout(name, shape):
        return nc.dram_tensor(name, list(shape), F32, kind="ExternalOutput").ap()

    x_d = din("x", [NTOK, D])
    p_d = din("p", [NTOK, PLE])
    sc_d = din("sc", [32, 512])
    grows_d = din("grows", [NGROWS, 128])
    w1g_d = din("w1g", [D, FF]); w1u_d = din("w1u", [D, FF]); w1d_d = din("w1d", [FF, D])
    w2g_d = din("w2g", [D, FF]); w2u_d = din("w2u", [D, FF]); w2d_d = din("w2d", [FF, D])
    win_d = din("win", [D, 2560])
    wout_d = din("wout", [D, D])
    wpg_d = din("wpg", [D, D])
    wpp_d = din("wpp", [PLE, D])
    ws_d = din("ws", [8, 128, 128])
    bs_d = din("bs", [8, 128])

    y_d = dout("y", [NTOK, D])
    ncp_d = dout("ncp", [2, 512])
    ncs_d = dout("ncs", [32, 512])
    cvp_d = dout("cvp", [128, 512])
    cvs_d = dout("cvs", [128, 512])

    SP = Eng(nc, nc.sync, "sp")
    PE = Eng(nc, nc.tensor, "pe")
    ACT = Eng(nc, nc.scalar, "act")
    DVE = Eng(nc, nc.vector, "dve")
    POOL = Eng(nc, nc.gpsimd, "pool")

    h_t = nc.alloc_sbuf_tensor("h", [128, KC, SMAX], F32)
    yn_t = nc.alloc_sbuf_tensor("yn", [128, 9216], F32)
    big_t = nc.alloc_sbuf_tensor("big", [128, NFF * SMAX], BF16)
    ring_t = nc.alloc_sbuf_tensor("ring", [128, NSLOT, SLOT_EL], BF16)
    pT_t = nc.alloc_sbuf_tensor("pT", [128, 2, SMAX], BF16)
    sq_t = nc.alloc_sbuf_tensor("sq", [128, 4, 512], BF16)
    rs_t = nc.alloc_sbuf_tensor("rs", [128, 4, 512], F32)
    sg_t = nc.alloc_sbuf_tensor("sg", [128, 2, 512], F32)
    xs_t = nc.alloc_sbuf_tensor("xs", [128, 2, D], F32)
    pst_t = nc.alloc_sbuf_tensor("pst", [128, 2, PLE], F32)
    vnT_t = nc.alloc_sbuf_tensor("vnT", [128, 2, 512], BF16)
    ident_t = nc.alloc_sbuf_tensor("ident", [128, 128], F32)
    ones_t = nc.alloc_sbuf_tensor("ones", [128, 128], BF16)
    bones_t = nc.alloc_sbuf_tensor("bones", [128, 128], BF16)
    wmT_t = nc.alloc_sbuf_tensor("wmT", [128, 8, 128], BF16)
    wmTs_t = nc.alloc_sbuf_tensor("wmTs", [128, 8, 128], BF16)
    bbc_t = nc.alloc_sbuf_tensor("bbc", [128, 4, 128], F32)
    bbcs_t = nc.alloc_sbuf_tensor("bbcs", [128, 4, 128], F32)
    gT_t = nc.alloc_sbuf_tensor("gT", [128, NGCOLS], F32)
    mask_t = nc.alloc_sbuf_tensor("mask", [128, 128], F32)
    mbd_t = nc.alloc_sbuf_tensor("mbd", [128, 128], F32)
    R_t = nc.alloc_sbuf_tensor("R", [8, 128], F32)
    carry_t = nc.alloc_sbuf_tensor("carry", [128, 4, 2], F32)
    ncv_t = nc.alloc_sbuf_tensor("ncv", [128, 4, 32], F32)

    bigf = big_t[:].bitcast(F32)
    ynb = yn_t[:].bitcast(BF16)
    hmid = big_t[:].rearrange("p (j s) -> p j s", j=NFF)
    zp = bigf[:, 0:4104].rearrange("p (c s) -> p c s", c=4)
    zs = bigf[:, 4104:4744].rearrange("p (c q s) -> p c q s", c=4, q=16)
    yab = bigf[:, 4744:4744 + 4608].rearrange("p (c s) -> p c s", c=4)
    ynm = big_t[:, 0:9216].rearrange("p (c s) -> p c s", c=8)
    gbuf = bigf[:, 0:9216].rearrange("p (c s) -> p c s", c=8)
    grows_v = bigf[:, 12416:12544]
    wst_v = bigf[:, 128:128 + 1024].rearrange("p (h s) -> p h s", h=8)
    T1_v = bigf[:, 1152:1152 + 1024]
    scst_v = bigf[:, 2176:2176 + 512]
    wmf_v = bigf[:, 2688:2688 + 1024].rearrange("p (h s) -> p h s", h=8)
    vo_v = bigf[:, 10432:10432 + 512]
    nco_v = bigf[:, 9400:9400 + 1024].rearrange("p (a s) -> p a s", a=2)

    ring_banks = [Bank(nc.alloc_psum_tensor("pb%d" % i, [128, 512], F32)) for i in range(5)]
    stat_banks = [Bank(nc.alloc_psum_tensor("ps%d" % i, [128, 512], F32)) for i in range(3)]
    st = {"rb": 0, "sb": 0, "sq": 0, "rs": 0, "sg": 0}

    pending = []

    def flush():
        while pending:
            pending.pop(0)()

    def get_bank():
        b = ring_banks[st["rb"] % 5]
        st["rb"] += 1
        PE.wait(b.free)
        b.free = []
        return b

    def get_stat():
        b = stat_banks[st["sb"] % 3]
        st["sb"] += 1
        return b

    def mm_group(out_ap, pairs, wait=None):
        PE.wait(wait)
        n = len(pairs)
        ins = None
        for i, (l, r) in enumerate(pairs):
            ins = nc.tensor.matmul(out_ap, l, r, start=(i == 0), stop=(i == n - 1))
        ev = PE.mark(ins)
        flush()
        return ev

    def transposes(out_aps, in_aps, wait=None):
        PE.wait(wait)
        ins = None
        for o, i_ in zip(out_aps, in_aps):
            ins = nc.tensor.transpose(o, i_, ident_t[0:i_.shape[0], 0:i_.shape[0]])
        return PE.mark(ins)

    sq_free = [None] * 4

    def sq_chunk():
        i = st["sq"] % 4
        st["sq"] += 1
        return i

    rs_free = [None] * 4

    def rs_buf():
        i = st["rs"] % 4
        st["rs"] += 1
        return i

    def act(wait, **kw):
        ACT.wait(wait)
        return ACT.mark(nc.scalar.activation(**kw))

    slot_sem = [DmaSem(nc, "slot%d" % i) for i in range(NSLOT)]
    slot_rel = [None] * NSLOT
    fills = []
    fill_ev = {}
    wr = {"issued": 0}

    MAXFLY = 2

    def issue_next():
        f = wr["issued"]
        if f >= len(fills):
            return
        s = f % NSLOT
        POOL.wait(slot_rel[s], fill_ev.get(f - MAXFLY))
        ev = None
        for ins in fills[f](ring_t[:, s, :]):
            ev = slot_sem[s].add(ins)
        fill_ev[f] = ev
        wr["issued"] += 1

    def release(f, ev):
        slot_rel[f % NSLOT] = ev
        issue_next()

    def wdma(out, in_):
        return nc.gpsimd.dma_start(out=out, in_=in_)

    def kview(w_d):
        return w_d.rearrange("(k p) n -> p k n", p=128)

    def fill_gu(wg, wu, j0):
        def f(sv):
            g = sv[:, 0:2048].rearrange("p (k n) -> p k n", k=8)
            u = sv[:, 2048:4096].rearrange("p (k n) -> p k n", k=8)
            return [wdma(g, kview(wg)[:, :, j0 * 128:(j0 + 2) * 128]),
                    wdma(u, kview(wu)[:, :, j0 * 128:(j0 + 2) * 128])]
        return f

    def fill_cols(w_d, nk, c0, ncol):
        def f(sv):
            v = sv[:, 0:nk * ncol].rearrange("p (k n) -> p k n", k=nk)
            if nk > 8:
                hk = nk // 2
                return [wdma(v[:, 0:hk, :], kview(w_d)[:, 0:hk, c0:c0 + ncol]),
                        wdma(v[:, hk:nk, :], kview(w_d)[:, hk:nk, c0:c0 + ncol])]
            return [wdma(v, kview(w_d)[:, :, c0:c0 + ncol])]
        return f

    IN_ORDER = [1, 2, 4, 0, 3]
    for _st in range(2):
        ntl = 2
        for wset in ((w1g_d, w1u_d, w1d_d), None, (w2g_d, w2u_d, w2d_d)):
            if wset is None:
                for blk in IN_ORDER:
                    fills.append(fill_cols(win_d, 8, blk * 512, 512))
                for _t in range(ntl):
                    fills.append(fill_cols(wout_d, 8, 0, 512))
                    fills.append(fill_cols(wout_d, 8, 512, 512))
                continue
            wg, wu, wd = wset
            for j0 in range(0, NFF, 2):
                fills.append(fill_gu(wg, wu, j0))
            for _t in range(ntl):
                for m in range(8):
                    fills.append(fill_cols(wd, NFF, m * 128, 128))
        for _t in range(ntl):
            fills.append(fill_cols(wpg_d, 8, 0, 512))
            fills.append(fill_cols(wpg_d, 8, 512, 512))
            fills.append(fill_cols(wpp_d, 2, 0, 1024))
    fcur = {"i": 0}

    def next_fill():
        f = fcur["i"]
        fcur["i"] += 1
        return f, fill_ev[f]

    ld_g = DmaSem(nc, "ld_gain")
    ld = DmaSem(nc, "ld_const")
    pc = {"ev": None}

    def pchain(ins):
        pc["ev"] = POOL.mark(ins)
        POOL.wait(pc["ev"])
        return pc["ev"]

    ev_gz = pchain(nc.gpsimd.memset(grows_v, 0.0))
    SP.wait(ev_gz)
    ev_ldg = ld_g.add(nc.sync.dma_start(out=grows_v[0:NGROWS, :], in_=grows_d))
    pchain(nc.gpsimd.memset(ident_t[:], 0.0))
    pchain(nc.gpsimd.affine_select(out=ident_t[:], in_=ident_t[:], compare_op=ALU.not_equal, fill=1.0, base=0,
                                   pattern=[[-1, 128]], channel_multiplier=1))
    pchain(nc.gpsimd.memset(ones_t[:], 1.0 / 1024.0))
    pchain(nc.gpsimd.memset(bones_t[:], 0.0))
    pchain(nc.gpsimd.memset(bones_t[0:64, 0:64], 1.0 / 64.0))
    ev_pc0 = pchain(nc.gpsimd.memset(bones_t[64:128, 64:128], 1.0 / 64.0))
    for _ in range(NSLOT):
        issue_next()
    pchain(nc.gpsimd.memset(mask_t[:], 1.0))
    pchain(nc.gpsimd.affine_select(out=mask_t[:], in_=mask_t[:], compare_op=ALU.is_ge, fill=0.0, base=0,
                                   pattern=[[1, 128]], channel_multiplier=-1))
    mbd3 = mbd_t[:].rearrange("p (q s) -> p q s", q=16)
    pchain(nc.gpsimd.memset(mbd_t[:], 1.0))
    pchain(nc.gpsimd.affine_select(out=mbd3, in_=mbd3, compare_op=ALU.is_ge, fill=0.0, base=0,
                                   pattern=[[-8, 16], [0, 8]], channel_multiplier=1))
    pchain(nc.gpsimd.affine_select(out=mbd3, in_=mbd3, compare_op=ALU.is_ge, fill=0.0, base=7,
                                   pattern=[[8, 16], [0, 8]], channel_multiplier=-1))
    R3 = R_t[:].rearrange("p (q s) -> p q s", q=16)
    pchain(nc.gpsimd.memset(R_t[:], 0.0))
    pchain(nc.gpsimd.affine_select(out=R3, in_=R3, compare_op=ALU.not_equal, fill=1.0, base=0,
                                   pattern=[[0, 16], [1, 8]], channel_multiplier=-1))
    ev_pc = pchain(nc.gpsimd.memset(carry_t[:], 0.0))

    b0 = get_bank()
    ev = transposes([b0.t[:, 0:128]], [grows_v], wait=[ev_ldg, ev_pc0])
    DVE.wait(ev)
    ev_g = DVE.mark(nc.vector.tensor_copy(gT_t[:, 0:NGROWS], b0.t[:, 0:NGROWS]))
    DVE.wait(ev_g)
    nc.vector.tensor_scalar(out=gT_t[:, H1POST:H1POST + 8], in0=gT_t[:, G1POST:G1POST + 8], scalar1=0.5,
                            scalar2=None, op0=ALU.mult)
    ev_g = DVE.mark(nc.vector.tensor_scalar(out=gT_t[:, H2POST:H2POST + 8], in0=gT_t[:, G2POST:G2POST + 8],
                                            scalar1=0.5, scalar2=None, op0=ALU.mult))
    b0.free = [ev_g]
    ACT.wait(ev_g, ev_pc0)
    DVE.wait(ev_g, ev_pc0)

    def gcol(i):
        return gT_t[:, i:i + 1]

    sct_t = nc.alloc_sbuf_tensor("sct", [128, 4, 32], F32)
    late = {}

    wst_s = xs_t[:, 0, :].rearrange("p (h s) -> p h s", h=8)
    wmf_s = xs_t[:, 1, :].rearrange("p (h s) -> p h s", h=8)
    T1_s = sg_t[:].rearrange("p a s -> p (a s)")
    scst_s = pst_t[:].rearrange("p a s -> p (a s)")

    def late_loads():
        ld.add(nc.sync.dma_start(out=wst_s, in_=ws_d.rearrange("h t s -> t h s")))
        ld.add(nc.sync.dma_start(out=scst_s[0:32, :], in_=sc_d))
        for hh in range(8):
            c, half = hh // 2, hh % 2
            ld.add(nc.sync.dma_start(out=bbc_t[64 * half:64 * half + 64, c, :],
                                     in_=bs_d[hh:hh + 1, :].broadcast_to([64, 128])))
            ld.add(nc.sync.dma_start(
                out=bbcs_t[64 * half:64 * half + 64, c, :].rearrange("p (q s) -> p q s", q=16),
                in_=bs_d[hh:hh + 1, 0:8].unsqueeze(1).broadcast_to([64, 16, 8])))

    def late_stage1():
        ev_ld = Ev(ld.sem, ld.n, ld.name)
        DVE.wait(ev_pc)
        ACT.wait(ev_pc)
        ev_w = None
        evt_w = None
        for half in range(2):
            b = get_bank()
            evt_w = transposes([b.t[:, k * 128:(k + 1) * 128] for k in range(4)],
                               [wst_s[:, half * 4 + k, :] for k in range(4)], wait=[ev_ld])
            DVE.wait(evt_w)
            ev_w = DVE.mark(nc.vector.tensor_tensor(
                out=wmf_s[:, half * 4:half * 4 + 4, :], in0=b.t[:].rearrange("p (h s) -> p h s", h=4),
                in1=mask_t[:].unsqueeze(1).broadcast_to([128, 4, 128]), op=ALU.mult))
            b.free = [ev_w]
        DVE.wait(ev_w)
        ev_wm = DVE.mark(nc.vector.tensor_copy(wmT_t[:], wmf_s))
        late["wm"] = [ev_wm, ev_ld]
        late["wmf"] = ev_w
        b = get_bank()
        evt_s = transposes([b.t[:, c * 32:(c + 1) * 32] for c in range(4)],
                           [scst_s[0:32, c * 128:(c + 1) * 128] for c in range(4)], wait=[ev_ld])
        DVE.wait(evt_s)
        late["sct"] = DVE.mark(nc.vector.tensor_copy(sct_t[:], b.t[:, 0:128].rearrange("p (c s) -> p c s", c=4)))
        b.free = [late["sct"]]
        xs_free[0] = [evt_w]
        for sb in p_stage1:
            sb.free = evt_s

    def late_stage2():
        DVE.wait(sg_free, late["wmf"])
        T14 = T1_s[0:8, :].rearrange("p (h q s) -> p h q s", h=8, q=16)
        ev_t1 = DVE.mark(nc.vector.tensor_copy(T14, wmf_s[0:8, :, 0:8].unsqueeze(2).broadcast_to([8, 8, 16, 8])))
        ev_mm = None
        ev_w = None
        for half in range(2):
            b = get_bank()
            PE.wait(ev_t1, ev_pc)
            ev_mm = PE.mark(nc.tensor.matmul(b.t[:], R_t[:], T1_s[0:8, half * 512:(half + 1) * 512],
                                             start=True, stop=True))
            DVE.wait(ev_mm)
            ev_w = DVE.mark(nc.vector.tensor_tensor(
                out=wmTs_t[:, half * 4:half * 4 + 4, :], in0=b.t[:].rearrange("p (h s) -> p h s", h=4),
                in1=mbd_t[:].unsqueeze(1).broadcast_to([128, 4, 128]), op=ALU.mult))
            b.free = [ev_w]
        late["wms"] = [ev_w]
        xs_free[1] = [ev_t1]
        sg_free[0] = ev_mm
        sg_free[1] = ev_mm

    out_sem = DmaSem(nc, "out")
    st_sem = [DmaSem(nc, "st0"), DmaSem(nc, "st1")]
    vo_sem = DmaSem(nc, "vost")
    class Stage:
        def __init__(self, ap, name):
            self.ap = ap
            self.sem = DmaSem(nc, name)
            self.free = None

    x_stage0 = [Stage(bigf[:, k * 1024:(k + 1) * 1024], "xs0_%d" % k) for k in range(8)]
    p_stage0 = [Stage(bigf[:, 8192 + k * 256:8192 + (k + 1) * 256], "ps0_%d" % k) for k in range(8)]
    x_stage1 = [Stage(bigf[:, 9216 + k * 1024:9216 + (k + 1) * 1024], "xs1_%d" % k) for k in range(3)]
    p_stage1 = [Stage(pst_t[:, k, :], "ps1_%d" % k) for k in range(2)]
    stg = {"x": x_stage0, "p": p_stage0, "xi": 0, "pi": 0}
    xs_free = [None, None]
    ps_free = [None, None]
    big_free = {"ev": [ev_g]}
    cnt = {"blk": 0, "vn": 0}
    vnT_free = [None, None]
    vo_free = {"ev": None}

    def rstd_from(bank, w, ev_stat):
        a = rs_buf()
        ev_s = act([ev_stat, rs_free[a]], out=rs_t[:, a, 0:w], in_=bank.t[:, 0:w], func=AF.Ln, bias=EPS, scale=1.0)
        bank.free = [ev_s]
        r = rs_buf()
        ev_r = act([ev_s, rs_free[r]], out=rs_t[:, r, 0:w], in_=rs_t[:, a, 0:w], func=AF.Exp, scale=-0.5)
        rs_free[a] = ev_r
        return r, ev_r

    def stat_mm(bank, w, lhs, sqi, first, last, ev_in):
        def f():
            PE.wait(ev_in)
            if first:
                PE.wait(bank.free)
                bank.free = []
            ins = nc.tensor.matmul(bank.t[:, 0:w], lhs, sq_t[:, sqi, 0:w], start=first, stop=last)
            sq_free[sqi] = PE.mark(ins)
            if last:
                bank.done = sq_free[sqi]
        pending.append(f)

    def square_to_stat(src_ap, w, bank, first, last, ev_in, lhs=None, defer=True):
        i = sq_chunk()
        ev_q = act([ev_in, sq_free[i]], out=sq_t[:, i, 0:w], in_=src_ap, func=AF.Square)
        stat_mm(bank, w, ones_t[:] if lhs is None else lhs, i, first, last, ev_q)
        if not defer:
            flush()
        return ev_q

    class Tile:
        def __init__(self, idx, c0, w, g0, sample):
            self.idx, self.c0, self.w, self.g0, self.sample = idx, c0, w, g0, sample
            self.off = idx * 4096
            self.cols = slice(c0, c0 + w)
            self.h_ev = None
            self.n_ev = None

        def y(self, m):
            return yn_t[:, self.off + m * self.w: self.off + (m + 1) * self.w]

        def n(self, k):
            o = 2 * self.off
            return ynb[:, o + k * self.w: o + (k + 1) * self.w]

        def n_all(self):
            o = 2 * self.off
            return ynb[:, o: o + 8 * self.w].rearrange("p (k s) -> p k s", k=8)

        def vb(self, c):
            o = self.off + 4 * self.w
            return yn_t[:, o + c * self.w: o + (c + 1) * self.w]

        def vb_all(self):
            o = self.off + 4 * self.w
            return yn_t[:, o: o + 4 * self.w].rearrange("p (c s) -> p c s", c=4)

    def prenorm(t, gidx, split=False):
        hc = getattr(t, "h_chunk", None)
        evq = [act([hc[m] if hc else t.h_ev, t.n_ev], out=t.n(m), in_=h_t[:, m, t.cols], func=AF.Square)
               for m in range(8)]

        def rest():
            bank = get_stat()
            flush()
            PE.wait(bank.free)
            bank.free = []
            ins = None
            for m in range(8):
                PE.wait(evq[m])
                ins = nc.tensor.matmul(bank.t[:, 0:t.w], ones_t[:], t.n(m), start=(m == 0), stop=(m == 7))
            bank.done = PE.mark(ins)
            r, ev_r = rstd_from(bank, t.w, bank.done)
            ev = None
            for m in range(8):
                DVE.wait(ev_r, t.h_ev)
                ev = DVE.mark(nc.vector.scalar_tensor_tensor(
                    out=t.n(m), in0=h_t[:, m, t.cols], scalar=gcol(gidx + m), in1=rs_t[:, r, 0:t.w],
                    op0=ALU.mult, op1=ALU.mult))
            rs_free[r] = ev
            t.n_ev = ev

        if split:
            return rest
        rest()
        return None

    def postnorm_residual(t, bank, gidx, src):
        r, ev_r = rstd_from(bank, t.w, bank.done)
        ev = None
        ev1 = [None] * 8

        def add(m):
            DVE.wait(ev1[m])
            return DVE.mark(nc.vector.tensor_tensor(out=h_t[:, m, t.cols], in0=h_t[:, m, t.cols], in1=src(m),
                                                    op=ALU.add))

        t.h_chunk = {}
        for m in range(8):
            DVE.wait(ev_r, t.y_ev)
            ev1[m] = DVE.mark(nc.vector.scalar_tensor_tensor(
                out=src(m), in0=src(m), scalar=gcol(gidx + m), in1=rs_t[:, r, 0:t.w],
                op0=ALU.mult, op1=ALU.mult))
            if m >= 1:
                ev = add(m - 1)
                t.h_chunk[m - 1] = ev
        ev = add(7)
        t.h_chunk[7] = ev
        rs_free[r] = ev
        t.h_ev = ev

    pend_pre = []

    def fill_groups(tl):
        if len(tl) == 3:
            return [[tl[0]], [tl[1], tl[2]]]
        return [[t] for t in tl]

    pend_pre2 = []

    def run_pend_pre():
        while pend_pre:
            r_ = pend_pre.pop(0)()
            if r_ is not None:
                pend_pre2.append(r_)

    def run_pend_pre2():
        while pend_pre2:
            pend_pre2.pop(0)()

    def out_proj(tiles, nfill, ms_per_fill, nk, rhs, rhs_ev, gidx, next_pre):
        prev = None
        rest_ = None
        for grp in fill_groups(tiles):
            banks = {t.idx: get_stat() for t in grp}
            g = 0
            for fi in range(nfill):
                f, fev = next_fill()
                sv = ring_t[:, f % NSLOT, 0:nk * ms_per_fill * 128].rearrange("p (k n) -> p k n", k=nk)
                evl = None
                for mi in range(ms_per_fill):
                    m = fi * ms_per_fill + mi
                    for t in grp:
                        b = get_bank()
                        evl = mm_group(b.t[:, 0:t.w],
                                       [(sv[:, k, mi * 128:(mi + 1) * 128], rhs(t, k)) for k in range(nk)],
                                       wait=[fev, rhs_ev(t)])
                        ev_c = act([evl], out=t.y(m), in_=b.t[:, 0:t.w], func=AF.Copy)
                        ev_q = square_to_stat(b.t[:, 0:t.w], t.w, banks[t.idx], m == 0, m == 7, evl)
                        b.free = [ev_q]
                        t.y_ev = ev_c
                    g += 1
                    if prev is not None and g == (4 if nk > 8 else 7):
                        rest_ = next_pre(prev, True)
                        prev = None
                    elif rest_ is not None and nk > 8 and g == 6:
                        rest_()
                        rest_ = None
                release(f, evl)
            flush()
            if rest_ is not None:
                rest_()
                rest_ = None
            if prev is not None:
                next_pre(prev)
                prev = None
            for t in grp[:-1]:
                postnorm_residual(t, banks[t.idx], gidx, t.y)
                pend_pre.append(lambda last=t: next_pre(last, True))
            t = grp[-1]
            postnorm_residual(t, banks[t.idx], gidx, t.y)
            prev = t
        pend_pre.append(lambda last=prev: next_pre(last, True))

    KSKEW = 3

    def ffn(tiles, g_post_half, next_pre, before_b=None):
        hm = {"ev": None}

        def gu(f, fev, j0, t):
            sv = ring_t[:, f % NSLOT, :]
            wg = sv[:, 0:2048].rearrange("p (k n) -> p k n", k=8)
            wu = sv[:, 2048:4096].rearrange("p (k n) -> p k n", k=8)
            evl = None
            for jj in range(2):
                j = j0 + jj
                bg = get_bank()
                evg = mm_group(bg.t[:, 0:t.w], [(wg[:, k, jj * 128:(jj + 1) * 128], t.n(k)) for k in range(8)],
                               wait=[fev, t.n_ev])
                bu = get_bank()
                evl = mm_group(bu.t[:, 0:t.w], [(wu[:, k, jj * 128:(jj + 1) * 128], t.n(k)) for k in range(8)])
                si = st["sg"] % 2
                st["sg"] += 1
                ev_s = act([evg, sg_free[si]], out=sg_t[:, si, 0:t.w], in_=bg.t[:, 0:t.w], func=AF.Silu)
                bg.free = [ev_s]
                DVE.wait(ev_s, evl, big_free["ev"])
                hm["ev"] = DVE.mark(nc.vector.tensor_tensor(out=hmid[:, j, t.cols], in0=bu.t[:, 0:t.w],
                                                            in1=sg_t[:, si, 0:t.w], op=ALU.mult))
                bu.free = [hm["ev"]]
                sg_free[si] = hm["ev"]
            return evl

        slots = list(range(0, NFF, 2))
        head = [next_fill() for _ in range(KSKEW)]
        evl = None
        for ti, t in enumerate(tiles):
            for si, ((f, fev), j0) in enumerate(zip(head, slots[:KSKEW])):
                evl = gu(f, fev, j0, t)
                if ti == 0 and si == 0:
                    run_pend_pre()
                if ti == 0 and si == 1:
                    run_pend_pre()
                    run_pend_pre2()
        for (f, fev) in head:
            release(f, evl)
        for j0 in slots[KSKEW:]:
            f, fev = next_fill()
            for t in tiles:
                evl = gu(f, fev, j0, t)
            release(f, evl)
        if before_b is not None:
            before_b()
        out_proj(tiles, 8, 1, NFF, lambda t, j: hmid[:, j, t.cols], lambda t: hm["ev"], g_post_half, next_pre)
        big_free["ev"] = [PE.now()]

    sg_free = [None, None]

    all_tiles = [[Tile(0, 0, 512, 0, False), Tile(1, 512, 512, 512, False)],
                 [Tile(0, 0, 512, 1024, False), Tile(1, 512, 512, 1536, False), Tile(2, 1024, 128, 2048, True)]]

    def p0_tile(t, defer_pre=False):
        h_evs = []
        for bi in range(t.c0 // 128, (t.c0 + t.w) // 128):
            xsb = stg["x"][stg["xi"] % len(stg["x"])]
            stg["xi"] += 1
            psb = stg["p"][stg["pi"] % len(stg["p"])]
            stg["pi"] += 1
            r0 = t.g0 + bi * 128 - t.c0
            SP.wait(xsb.free, psb.free, stg.get("gate"))
            evx = xsb.sem.add(nc.sync.dma_start(out=xsb.ap, in_=x_d[r0:r0 + 128, :]))
            evp = psb.sem.add(nc.sync.dma_start(out=psb.ap, in_=p_d[r0:r0 + 128, :]))
            cs = slice(bi * 128, (bi + 1) * 128)
            evt = None
            for half in range(2):
                b = get_bank()
                evt = transposes([b.t[:, k * 128:(k + 1) * 128] for k in range(4)],
                                 [xsb.ap[:, (half * 4 + k) * 128:(half * 4 + k + 1) * 128] for k in range(4)],
                                 wait=[evx])
                src = b.t[:].rearrange("p (k s) -> p k s", k=4)
                if half == 0:
                    DVE.wait(evt)
                    e2 = DVE.mark(nc.vector.tensor_copy(h_t[:, 0:4, cs], src))
                else:
                    e2 = act([evt], out=h_t[:, 4:8, cs], in_=src, func=AF.Copy)
                b.free = [e2]
                h_evs.append(e2)
            xsb.free = evt
            b = get_bank()
            evt = transposes([b.t[:, k * 128:(k + 1) * 128] for k in range(2)],
                             [psb.ap[:, k * 128:(k + 1) * 128] for k in range(2)], wait=[evp])
            e2 = act([evt], out=pT_t[:, :, cs], in_=b.t[:, 0:256].rearrange("p (k s) -> p k s", k=2), func=AF.Copy)
            b.free = [e2]
            psb.free = evt
            stg["last"] = [xsb.free, psb.free]
            t.pT_ev = e2
        t.h_ev = list(h_evs)
        t.h_chunk = None
        t.y_ev = None
        if defer_pre:
            pend_pre.append(lambda t=t: prenorm(t, G1PRE, True))
        else:
            prenorm(t, G1PRE)

    try:
        for sti in range(2):
            if _STOP is not None and _STOP == "setup":
                break
            tiles = all_tiles[sti]
            if len(tiles) == 3:
                tiles = [tiles[0], tiles[2], tiles[1]]
            next_tiles = all_tiles[sti + 1] if sti == 0 else []
            g_base = tiles[0].g0
            if sti == 0:
                for t in tiles:
                    p0_tile(t, defer_pre=(t is not tiles[0]))
                big_free["ev"] = big_free["ev"] + [sb.free for sb in x_stage0 + p_stage0]
                late_loads()

            _chk('p0%d' % sti)
            ffn(tiles, H1POST, lambda t, sp=False: prenorm(t, GMPRE, sp), late_stage1 if sti == 0 else None)

            _chk('ffn1%d' % sti)
            ev_pe_now = PE.now()
            DVE.wait(big_free["ev"])
            DVE.wait(ev_pe_now)
            ACT.wait(ev_pe_now)
            ev_init = DVE.mark(nc.vector.tensor_copy(zp[:, :, 0:2], carry_t[:]))
            if sti == 1:
                DVE.wait(late["sct"])
                ev_init = DVE.mark(nc.vector.tensor_copy(
                    zs[:, :, :, 0:2], sct_t[:].rearrange("p c (q s) -> p c q s", q=16)))

            def zview(t, c, sh):
                if t.sample:
                    return zs[:, c, :, 2 - sh:10 - sh]
                return zp[:, c, 2 + t.c0 - sh: 2 + t.c0 + t.w - sh]

            def bview(t, ap2):
                if t.sample:
                    return ap2.rearrange("p (q s) -> p q s", q=16)
                return ap2

            def wview(f):
                return ring_t[:, f % NSLOT, :].rearrange("p (k n) -> p k n", k=8)

            def proj(sv, fev, t, c):
                b = get_bank()
                ev = mm_group(b.t[:, 0:t.w], [(sv[:, k, c * 128:(c + 1) * 128], t.n(k)) for k in range(8)],
                              wait=[fev, t.n_ev])
                return b, ev

            fCA, evCA = next_fill()
            fHA, evHA = next_fill()
            z_prev = ev_init
            conv_ev = {}
            z_evs = {}
            evl = None
            for t in tiles:
                ca = []
                for c in range(4):
                    b, evl = proj(wview(fCA), evCA, t, c)
                    e = act([evl, ev_init], out=zview(t, c, 0), in_=bview(t, b.t[:, 0:t.w]), func=AF.Copy)
                    b.free = [e]
                    ca.append(e)
                run_pend_pre()
                zev = []
                for c in range(4):
                    b, evl = proj(wview(fHA), evHA, t, c)
                    DVE.wait(evl, ca[c])
                    e = DVE.mark(nc.vector.tensor_tensor(out=zview(t, c, 0), in0=bview(t, b.t[:, 0:t.w]),
                                                         in1=zview(t, c, 0), op=ALU.mult))
                    b.free = [e]
                    zev.append(e)
                run_pend_pre()
                run_pend_pre2()
                ce = [None] * 4
                for c in range(4):
                    o = bview(t, yab[:, c, t.cols])
                    DVE.wait(zev, z_prev)
                    ce[c] = DVE.mark(nc.vector.tensor_scalar(out=o, in0=zview(t, c, 0),
                                                             scalar1=gcol(GCW + 2 * 4 + c), scalar2=None, op0=ALU.mult))
                for sh, kk in ((1, 1), (2, 0)):
                    for c in range(4):
                        o = bview(t, yab[:, c, t.cols])
                        DVE.wait(ce[c])
                        ce[c] = DVE.mark(nc.vector.scalar_tensor_tensor(
                            out=o, in0=zview(t, c, sh), scalar=gcol(GCW + kk * 4 + c), in1=o,
                            op0=ALU.mult, op1=ALU.add))
                e = ce[3]
                conv_ev[t.idx] = e
                z_evs[t.idx] = zev
                z_prev = zev
            release(fCA, evl)
            release(fHA, evl)
            DVE.wait(z_evs[1], z_prev, conv_ev[1])
            ev_carry = DVE.mark(nc.vector.tensor_copy(carry_t[:], zp[:, :, 1024:1026]))
            ev_zdone = [ev_carry, conv_ev[tiles[-1].idx]]
            if sti == 1:
                ev_nv = DVE.mark(nc.vector.tensor_copy(ncv_t[:].rearrange("p c (q s) -> p c q s", q=16),
                                                       zs[:, :, :, 8:10]))
                ev_zdone.append(ev_nv)
            fV, evV = next_fill()
            fBA, evBA = next_fill()
            fU, evU = next_fill()
            for t in tiles:
                evs = []

                def finish_v(c, bank, e_c, t=t, evs=evs):
                    r, ev_r = rstd_from(bank, t.w, bank.done)
                    DVE.wait(ev_r, e_c)
                    e = DVE.mark(nc.vector.scalar_tensor_tensor(
                        out=t.vb(c), in0=t.vb(c), scalar=gcol(GV + c), in1=rs_t[:, r, 0:t.w],
                        op0=ALU.mult, op1=ALU.mult))
                    rs_free[r] = e
                    evs.append(e)

                prev_v = None
                for c in range(4):
                    b, evl = proj(wview(fV), evV, t, c)
                    if prev_v is not None:
                        finish_v(*prev_v)
                    e_c = act([evl], out=t.vb(c), in_=b.t[:, 0:t.w], func=AF.Copy)
                    bank = get_stat()
                    e_q = square_to_stat(b.t[:, 0:t.w], t.w, bank, True, True, evl, lhs=bones_t[:])
                    b.free = [e_q]
                    prev_v = (c, bank, e_c)
                bank_a = get_stat()
                ya_ev = []
                for c in range(4):
                    b, evl = proj(wview(fBA), evBA, t, c)
                    if prev_v is not None:
                        finish_v(*prev_v)
                        prev_v = None
                    DVE.wait(evl, conv_ev[t.idx])
                    e = DVE.mark(nc.vector.tensor_tensor(out=yab[:, c, t.cols], in0=b.t[:, 0:t.w],
                                                         in1=yab[:, c, t.cols], op=ALU.mult))
                    b.free = [e]
                    ya_ev.append(e)
                    square_to_stat(yab[:, c, t.cols], t.w, bank_a, c == 0, c == 3, e)
                def blk_T(kb, t=t, evs=evs):
                    bs_ = slice(kb * 128, (kb + 1) * 128)
                    b = get_bank()
                    evt = transposes([b.t[:, c * 128:(c + 1) * 128] for c in range(4)],
                                     [t.vb(c)[:, bs_] for c in range(4)], wait=evs)
                    vi = cnt["vn"] % 2
                    cnt["vn"] += 1
                    e_v = act([evt, vnT_free[vi]], out=vnT_t[:, vi, :], in_=b.t[:], func=AF.Copy)
                    frees = [e_v]
                    is_out = (sti == 1) and (t.sample or (t.idx == 1 and kb == 3))
                    if is_out:
                        DVE.wait(evt, e_v, vo_free["ev"])
                        e_o = DVE.mark(nc.vector.tensor_copy(vo_v, b.t[:]))
                        frees.append(e_o)
                        SP.wait(e_o)
                        vo_free["ev"] = vo_sem.add(nc.sync.dma_start(out=cvs_d if t.sample else cvp_d, in_=vo_v))
                    b.free = frees
                    return kb, vi, e_v

                def blk_M(kb, vi, e_v, t=t):
                    bs_ = slice(kb * 128, (kb + 1) * 128)
                    b2 = get_bank()
                    PE.wait(e_v, late["wms"] if t.sample else late["wm"])
                    wm = wmTs_t if t.sample else wmT_t
                    ins = None
                    for hh in range(8):
                        c, half = hh // 2, hh % 2
                        ins = nc.tensor.matmul(b2.t[64 * half:64 * half + 64, c * 128:(c + 1) * 128],
                                               vnT_t[:, vi, hh * 64:(hh + 1) * 64], wm[:, hh, :],
                                               start=True, stop=True)
                    evm = PE.mark(ins)
                    vnT_free[vi] = evm
                    DVE.wait(evm)
                    e = DVE.mark(nc.vector.tensor_tensor(
                        out=t.vb_all()[:, :, bs_], in0=b2.t[:].rearrange("p (c s) -> p c s", c=4),
                        in1=(bbcs_t if t.sample else bbc_t)[:], op=ALU.add))
                    b2.free = [e]
                    t.mix_ev = e

                pT = None
                for kb in range(t.w // 128):
                    cur = blk_T(kb)
                    if pT is not None:
                        blk_M(*pT)
                    pT = cur
                blk_M(*pT)
                flush()
                r, ev_r = rstd_from(bank_a, t.w, bank_a.done)
                e = None
                for c in range(4):
                    DVE.wait(ev_r, ya_ev, ev_zdone)
                    e = DVE.mark(nc.vector.scalar_tensor_tensor(
                        out=ynm[:, c, t.cols], in0=yab[:, c, t.cols], scalar=gcol(GA + c), in1=rs_t[:, r, 0:t.w],
                        op0=ALU.mult, op1=ALU.mult))
                rs_free[r] = e
                t.yna_ev = e
                bank_b = get_stat()
                yb_ev = []
                for c in range(4):
                    b, evl = proj(wview(fU), evU, t, c)
                    DVE.wait(evl, t.mix_ev)
                    e = DVE.mark(nc.vector.tensor_tensor(out=t.vb(c), in0=b.t[:, 0:t.w], in1=t.vb(c), op=ALU.mult))
                    b.free = [e]
                    yb_ev.append(e)
                    square_to_stat(t.vb(c), t.w, bank_b, c == 0, c == 3, e)
                flush()
                r, ev_r = rstd_from(bank_b, t.w, bank_b.done)
                e = None
                for c in range(4):
                    DVE.wait(ev_r, yb_ev, ev_zdone)
                    e = DVE.mark(nc.vector.scalar_tensor_tensor(
                        out=ynm[:, 4 + c, t.cols], in0=t.vb(c), scalar=gcol(GB + c), in1=rs_t[:, r, 0:t.w],
                        op0=ALU.mult, op1=ALU.mult))
                rs_free[r] = e
                t.ynm_ev = [t.yna_ev, e]
            release(fV, evl)
            release(fBA, evl)
            release(fU, evl)
            if sti == 1:
                b = get_bank()
                evt = transposes([b.t[0:2, c * 128:(c + 1) * 128] for c in range(4)],
                                 [carry_t[:, c, :] for c in range(4)], wait=[ev_carry])
                e = act([evt], out=nco_v[0:2, 0, :], in_=b.t[0:2, :], func=AF.Copy)
                b.free = [e]
                SP.wait(e)
                out_sem.add(nc.sync.dma_start(out=ncp_d, in_=nco_v[0:2, 0, :]))
                b = get_bank()
                evt = transposes([b.t[0:32, c * 128:(c + 1) * 128] for c in range(4)],
                                 [ncv_t[:, c, :] for c in range(4)], wait=[ev_nv])
                e = act([evt], out=nco_v[0:32, 1, :], in_=b.t[0:32, :], func=AF.Copy)
                b.free = [e]
                SP.wait(e)
                out_sem.add(nc.sync.dma_start(out=ncs_d, in_=nco_v[0:32, 1, :]))

            out_proj(tiles, 2, 4, 8, lambda t, k: ynm[:, k, t.cols], lambda t: t.ynm_ev, GMPOST,
                     lambda t, sp=False: prenorm(t, G2PRE, sp))
            big_free["ev"] = [PE.now()]
            if out_sem.n:
                big_free["ev"].append(Ev(out_sem.sem, out_sem.n, out_sem.name))
            if vo_sem.n:
                big_free["ev"].append(Ev(vo_sem.sem, vo_sem.n, vo_sem.name))

            _chk('mix%d' % sti)
            def ple_cast(t, sp=False):
                t.n_ev = act([t.h_ev, t.n_ev], out=t.n_all(), in_=h_t[:, :, t.cols], func=AF.Copy)
                return None

            ffn(tiles, H2POST, ple_cast, late_stage2 if sti == 0 else None)

            _chk('ffn2%d' % sti)
            ev_pe_now = PE.now()

            def out_tile(t):
                for bi in range(t.c0 // 128, (t.c0 + t.w) // 128):
                    i = cnt["blk"] % 2
                    cnt["blk"] += 1
                    r0 = g_base + bi * 128
                    cs = slice(bi * 128, (bi + 1) * 128)
                    e1 = e2 = None
                    for half in range(2):
                        b = get_bank()
                        evt = transposes([b.t[:, k * 128:(k + 1) * 128] for k in range(4)],
                                         [h_t[:, half * 4 + k, cs] for k in range(4)], wait=[t.h_ev])
                        dst = xs_t[:, i, half * 512:(half + 1) * 512]
                        if half == 0:
                            DVE.wait(evt, xs_free[i])
                            e1 = DVE.mark(nc.vector.tensor_copy(dst, b.t[:]))
                            b.free = [e1]
                        else:
                            e2 = act([evt, xs_free[i]], out=dst, in_=b.t[:], func=AF.Copy)
                            b.free = [e2]
                    ACT.wait(e1, e2)
                    xs_free[i] = st_sem[i].add(nc.scalar.dma_start(out=y_d[r0:r0 + 128, :], in_=xs_t[:, i, :]))

            ple_tiles = sorted(tiles, key=lambda t: t.idx)
            if next_tiles:
                stg.update({"x": x_stage1, "p": p_stage1, "xi": 0, "pi": 0, "gate": ev_pe_now})
            prev = []
            for grp in fill_groups(ple_tiles):
                gate_ev = {}
                for fi in range(2):
                    f, fev = next_fill()
                    sv = ring_t[:, f % NSLOT, :].rearrange("p (k n) -> p k n", k=8)
                    for mi in range(4):
                        m = fi * 4 + mi
                        for t in grp:
                            b = get_bank()
                            evl = mm_group(b.t[:, 0:t.w],
                                           [(sv[:, k, mi * 128:(mi + 1) * 128], t.n(k)) for k in range(8)],
                                           wait=[fev, t.n_ev])
                            e = act([evl, ev_pe_now], out=gbuf[:, m, t.cols], in_=b.t[:, 0:t.w], func=AF.Sigmoid)
                            b.free = [e]
                            gate_ev[(t.idx, m)] = e
                    release(f, evl)
                    run_pend_pre()
                f, fev = next_fill()
                sv = ring_t[:, f % NSLOT, 0:2048].rearrange("p (k n) -> p k n", k=2)
                banks = {t.idx: get_stat() for t in grp}
                for m in range(8):
                    for t in grp:
                        b = get_bank()
                        evl = mm_group(b.t[:, 0:t.w],
                                       [(sv[:, k, m * 128:(m + 1) * 128], pT_t[:, k, t.cols]) for k in range(2)],
                                       wait=[fev, t.pT_ev])
                        DVE.wait(evl, gate_ev[(t.idx, m)])
                        e = DVE.mark(nc.vector.tensor_tensor(out=gbuf[:, m, t.cols], in0=b.t[:, 0:t.w],
                                                             in1=gbuf[:, m, t.cols], op=ALU.mult))
                        b.free = [e]
                        square_to_stat(gbuf[:, m, t.cols], t.w, banks[t.idx], m == 0, m == 7, e)
                        t.y_ev = e
                release(f, evl)
                flush()
                for t in grp:
                    postnorm_residual(t, banks[t.idx], GPLE, lambda m, t=t: gbuf[:, m, t.cols])
                for pt in prev:
                    out_tile(pt)
                    if next_tiles:
                        p0_tile(next_tiles[pt.idx])
                prev = grp
            big_free["ev"] = [t.h_ev for t in tiles]
            for pt in prev:
                out_tile(pt)
            for nt_ in next_tiles[prev[-1].idx:]:
                p0_tile(nt_, defer_pre=(nt_.idx != 0))
            if next_tiles:
                big_free["ev"] = big_free["ev"] + [sb.free for sb in x_stage1]
            _chk('ple%d' % sti)

    except _Stop:
        flush()

    for ds in [out_sem, vo_sem] + st_sem:
        SP.wait(Ev(ds.sem, ds.n, ds.name))
    assert _STOP is not None or fcur["i"] == len(fills), (fcur["i"], len(fills))
    return nc


def build_two_pass():
    global _COLLECT, _USED
    _COLLECT, _USED = set(), None
    build_program()
    _USED, _COLLECT = _COLLECT, None
    try:
        return build_program()
    finally:
        _USED = None


_CACHE = {}


def kernel(**inputs):
    f32 = np.float32
    g = lambda k: np.asarray(inputs[k], dtype=f32)
    xp, xsm = g("x_prompt"), g("x_sample")
    pp, psm = g("p_prompt")[0], g("p_sample")[0]
    scv = g("state_conv")[0]
    grows = np.concatenate([
        g("ffn1_pre_g")[0].reshape(8, 128), g("ffn1_post_g")[0].reshape(8, 128),
        g("mix_pre_g")[0].reshape(8, 128), g("mix_post_g")[0].reshape(8, 128),
        g("conv_w")[0].reshape(12, 128), g("v_norm_g")[0].reshape(4, 128),
        g("out_g_a")[0].reshape(4, 128), g("out_g_b")[0].reshape(4, 128),
        g("ffn2_pre_g")[0].reshape(8, 128), g("ffn2_post_g")[0].reshape(8, 128),
        g("ple_post_g")[0].reshape(8, 128)], axis=0)
    assert grows.shape == (NGROWS, 128)
    shared = {
        "grows": np.ascontiguousarray(grows),
        "w1g": g("ffn1_w_gate")[0], "w1u": g("ffn1_w_up")[0], "w1d": g("ffn1_w_down")[0],
        "w2g": g("ffn2_w_gate")[0], "w2u": g("ffn2_w_up")[0], "w2d": g("ffn2_w_down")[0],
        "win": g("w_in")[0], "wout": g("w_out")[0], "wpg": g("ple_w_gate")[0], "wpp": g("ple_w_proj")[0],
        "ws": g("w_s")[0], "bs": g("b_s")[0],
    }
    shared = {k: np.ascontiguousarray(v) for k, v in shared.items()}
    in_maps = []
    for c in range(NCORES):
        m = dict(shared)
        m["x"] = np.ascontiguousarray(np.concatenate([xp[c], xsm[16 * c:16 * c + 16].reshape(128, D)], axis=0))
        m["p"] = np.ascontiguousarray(np.concatenate([pp[c], psm[16 * c:16 * c + 16].reshape(128, PLE)], axis=0))
        m["sc"] = np.ascontiguousarray(scv[16 * c:16 * c + 16].reshape(32, 512))
        in_maps.append(m)
    if "nc" not in _CACHE:
        _CACHE["nc"] = build_two_pass()
    res = run_bass_kernel_spmd(_CACHE["nc"], in_maps, core_ids=list(range(NCORES)))
    R = res.results
    y_prompt = np.stack([np.asarray(R[c]["y"])[:SEQ] for c in range(NCORES)]).astype(f32)
    y_sample = np.concatenate([np.asarray(R[c]["y"])[SEQ:].reshape(16, 8, D) for c in range(NCORES)]).astype(f32)
    ncp = np.stack([np.asarray(R[c]["ncp"]) for c in range(NCORES)])[None].astype(f32)
    ncs = np.concatenate([np.asarray(R[c]["ncs"]).reshape(16, 2, 512) for c in range(NCORES)])[None].astype(f32)
    cvp = np.stack([np.asarray(R[c]["cvp"]).reshape(128, 8, 64) for c in range(NCORES)])[None].astype(f32)
    cvs = np.concatenate([np.asarray(R[c]["cvs"]).reshape(16, 8, 8, 64) for c in range(NCORES)])[None].astype(f32)
    return (y_prompt, y_sample, ncp, ncs, cvp, cvs)
```
